# Optimizing a Trainium2 kernel written in Bass

```python
import jax, jax.numpy as jnp
from jax import lax
import numpy as np

D_MODEL = 1024
BATCH = 8
SEQ = 4096
DEPTH = 4

N_META = 16
N_MIXERS = 3
D_FF = 2816
NORMS_PER_LAYER = 6
RMS_EPS = 1e-6
CONV_WIDTH = 3
GLA_HEADS = 4
GLA_DK = D_MODEL // 2
GLA_DV = D_MODEL
GLA_HK = GLA_DK // GLA_HEADS
GLA_HV = GLA_DV // GLA_HEADS
GLA_RANK = 16
GLA_GATE_NORMALIZER = 16.0
GLA_CHUNK = 64
GLA_IN = 2 * GLA_DK + 2 * GLA_DV + GLA_RANK
SB_HEADS = 16
SB_HEAD_DIM = D_MODEL // SB_HEADS
SB_BLOCK = 128
N_A = (DEPTH + 2) // 3
N_B = (DEPTH + 1) // 3
N_C = DEPTH // 3

kernel_name = 'hybrid_conv_gla_stickbreaking_macaron'


def rmsnorm(x, g):
    xf = x.astype(jnp.float32)
    xf = xf * lax.rsqrt(jnp.mean(xf * xf, axis=-1, keepdims=True) + RMS_EPS)
    return (xf * g.astype(jnp.float32)).astype(x.dtype)


def swiglu(x, w_in, w_out):
    gate, up = jnp.split(x @ w_in, 2, axis=-1)
    return (jax.nn.silu(gate) * up) @ w_out


def short_conv_mixer(x, w_in, conv_w, w_out):
    L = x.shape[1]
    b, c, h = jnp.split(x @ w_in, 3, axis=-1)
    u = c * h
    up = jnp.pad(u, ((0, 0), (CONV_WIDTH - 1, 0), (0, 0)))
    conv = sum(conv_w[tap] * up[:, tap:tap + L] for tap in range(CONV_WIDTH))
    return (b * conv) @ w_out


def gla_mixer(x, w_in, w_gate_up, b_gate, norm_g, w_out):
    Bsz, L, _ = x.shape
    f32 = jnp.float32
    q, k, v, g, a = jnp.split(x @ w_in, [GLA_DK, 2 * GLA_DK, 2 * GLA_DK + GLA_DV, 2 * GLA_DK + 2 * GLA_DV], axis=-1)
    log_a = jax.nn.log_sigmoid((a @ w_gate_up + b_gate).astype(f32)) / GLA_GATE_NORMALIZER
    pad = GLA_CHUNK - N_META
    padf = lambda t: jnp.pad(t.astype(f32), ((0, 0), (pad, 0), (0, 0)))
    q, k, v, log_a = padf(q) * (GLA_HK ** -0.5), padf(k), padf(v), padf(log_a)
    n_chunks = (L + pad) // GLA_CHUNK

    def chunked(t, hd):
        return t.reshape(Bsz, n_chunks, GLA_CHUNK, GLA_HEADS, hd).transpose(1, 0, 3, 2, 4)

    qc, kc, vc = chunked(q, GLA_HK), chunked(k, GLA_HK), chunked(v, GLA_HV)
    bc = jnp.cumsum(chunked(log_a, GLA_HK), axis=3)
    causal = jnp.tril(jnp.ones((GLA_CHUNK, GLA_CHUNK), bool))[:, :, None]

    def step(S, inp):
        qi, ki, vi, bi = inp
        diff = bi[:, :, :, None, :] - bi[:, :, None, :, :]
        decay = jnp.exp(jnp.where(causal, diff, -jnp.inf))
        att = jnp.einsum('bhtd,bhsd,bhtsd->bhts', qi, ki, decay)
        o = jnp.einsum('bhts,bhsv->bhtv', att, vi) + jnp.einsum('bhtd,bhdv->bhtv', qi * jnp.exp(bi), S)
        b_last = bi[:, :, -1:, :]
        S = jnp.exp(b_last[:, :, 0, :, None]) * S + jnp.einsum('bhsd,bhsv->bhdv', ki * jnp.exp(b_last - bi), vi)
        return S, o

    S0 = jnp.zeros((Bsz, GLA_HEADS, GLA_HK, GLA_HV), f32)
    _, o = lax.scan(step, S0, (qc, kc, vc, bc))
    o = o.transpose(1, 0, 3, 2, 4).reshape(Bsz, n_chunks * GLA_CHUNK, GLA_HEADS, GLA_HV)[:, pad:]
    o = rmsnorm(o, norm_g).reshape(Bsz, L, GLA_DV).astype(x.dtype)
    return (o * jax.nn.silu(g)) @ w_out


def stick_breaking_mixer(x, w_in, w_out):
    Bsz, L, _ = x.shape
    f32 = jnp.float32
    q, k, v = jnp.split(x @ w_in, 3, axis=-1)
    heads = lambda t: t.reshape(Bsz, L, SB_HEADS, SB_HEAD_DIM).transpose(0, 2, 1, 3)
    q = heads(q).astype(f32) * (SB_HEAD_DIM ** -0.5)
    k, v = heads(k).astype(f32), heads(v)
    bounds = [(0, N_META)] + [(N_META + i * SB_BLOCK, N_META + (i + 1) * SB_BLOCK)
                              for i in range((L - N_META) // SB_BLOCK)]
    outs = []
    for t0, t1 in bounds:
        z = jnp.einsum('bhtd,bhsd->bhts', q[:, :, t0:t1], k[:, :, :t1])
        strict = jnp.arange(t1)[None, :] < jnp.arange(t0, t1)[:, None]
        log_1m = jnp.where(strict, jax.nn.log_sigmoid(-z), 0.0)
        tail = lax.cumsum(log_1m, axis=3, reverse=True) - log_1m
        w = jnp.where(strict, jnp.exp(jax.nn.log_sigmoid(z) + tail), 0.0)
        outs.append(jnp.einsum('bhts,bhsd->bhtd', w.astype(v.dtype), v[:, :, :t1]))
    o = jnp.concatenate(outs, axis=2).transpose(0, 2, 1, 3).reshape(Bsz, L, D_MODEL)
    return o @ w_out


def setup_inputs(seed: int = 0) -> dict:
    key = jax.random.key(seed)
    ks = jax.random.split(key, 16)
    f32 = jnp.float32
    nrm = lambda kk, shape, fan_in: jax.random.normal(kk, shape, f32) * (fan_in ** -0.5)
    return {
        'x': jax.random.normal(ks[0], (BATCH, SEQ, D_MODEL), f32),
        'meta_tokens': jax.random.normal(ks[1], (N_META, D_MODEL), f32),
        'norm_gains': 1.0 + 0.05 * jax.random.normal(ks[2], (DEPTH, NORMS_PER_LAYER, D_MODEL), f32),
        'ffn_w_in': nrm(ks[3], (DEPTH, 2, D_MODEL, 2 * D_FF), D_MODEL),
        'ffn_w_out': nrm(ks[4], (DEPTH, 2, D_FF, D_MODEL), D_FF),
        'conv_w_in': nrm(ks[5], (N_A, D_MODEL, 3 * D_MODEL), D_MODEL),
        'conv_w': nrm(ks[6], (N_A, CONV_WIDTH, D_MODEL), CONV_WIDTH),
        'conv_w_out': nrm(ks[7], (N_A, D_MODEL, D_MODEL), D_MODEL),
        'gla_w_in': nrm(ks[8], (N_B, D_MODEL, GLA_IN), D_MODEL),
        'gla_w_gate_up': nrm(ks[9], (N_B, GLA_RANK, GLA_DK), GLA_RANK),
        'gla_b_gate': 0.1 * jax.random.normal(ks[10], (N_B, GLA_DK), f32),
        'gla_norm': 1.0 + 0.05 * jax.random.normal(ks[11], (N_B, GLA_HV), f32),
        'gla_w_out': nrm(ks[12], (N_B, GLA_DV, D_MODEL), GLA_DV),
        'sb_w_in': nrm(ks[13], (N_C, D_MODEL, 3 * D_MODEL), D_MODEL),
        'sb_w_out': nrm(ks[14], (N_C, D_MODEL, D_MODEL), D_MODEL),
    }


def reference(x, meta_tokens, norm_gains, ffn_w_in, ffn_w_out, conv_w_in, conv_w, conv_w_out,
              gla_w_in, gla_w_gate_up, gla_b_gate, gla_norm, gla_w_out, sb_w_in, sb_w_out):
    Bsz = x.shape[0]
    meta = jnp.broadcast_to(meta_tokens[None].astype(x.dtype), (Bsz, N_META, D_MODEL))
    h = jnp.concatenate([meta, x], axis=1)
    for i in range(DEPTH):
        g = norm_gains[i]
        h = h + 0.5 * rmsnorm(swiglu(rmsnorm(h, g[0]), ffn_w_in[i, 0], ffn_w_out[i, 0]), g[1])
        kind, j = i % N_MIXERS, i // N_MIXERS
        u = rmsnorm(h, g[2])
        if kind == 0:
            m = short_conv_mixer(u, conv_w_in[j], conv_w[j], conv_w_out[j])
        elif kind == 1:
            m = gla_mixer(u, gla_w_in[j], gla_w_gate_up[j], gla_b_gate[j], gla_norm[j], gla_w_out[j])
        else:
            m = stick_breaking_mixer(u, sb_w_in[j], sb_w_out[j])
        h = h + rmsnorm(m, g[3])
        h = h + 0.5 * rmsnorm(swiglu(rmsnorm(h, g[4]), ffn_w_in[i, 1], ffn_w_out[i, 1]), g[5])
    return h[:, N_META:]
```

```python
import numpy as np
import concourse.bass as bass
import concourse.mybir as mybir
from concourse.bass_utils import run_bass_kernel_spmd
from contextlib import ExitStack

F32 = mybir.dt.float32
BF16 = mybir.dt.bfloat16
ALU = mybir.AluOpType
AF = mybir.ActivationFunctionType

D = 1024
DFF = 2816
NMETA = 16
EPS = 1e-6
NCORES = 8
TT = 384
NRING = 8


class Buf:
    __slots__ = ("w", "r", "dsem", "name")

    def __init__(self, name=""):
        self.w = None
        self.r = {}
        self.dsem = None
        self.name = name


class Sched:
    CE = ("pe", "act", "dve", "pool")

    def __init__(self, nc):
        self.nc = nc
        self.E = {"pe": nc.tensor, "act": nc.scalar, "dve": nc.vector, "pool": nc.gpsimd, "sp": nc.sync}
        self.sems = []
        self.isdma = []
        self.dissued = {}
        self.esem = {}
        self.cnt = {}
        self.pending = {e: False for e in self.CE}
        self.seen = {e: {} for e in self.E}
        self.bufs = []
        self.nwait = 0
        self.nins = 0
        self.pool_ld = [self.new_sem("ld%d" % i, True) for i in range(8)]
        self.pool_st = [self.new_sem("st%d" % i, True) for i in range(8)]
        self.rr_ld = 0
        self.rr_st = 0
        self.epoch = 0
        self._new_epoch_sems()

    def new_sem(self, name, dma=False):
        h = self.nc.alloc_semaphore(name)
        self.sems.append(h)
        self.isdma.append(dma)
        i = len(self.sems) - 1
        if dma:
            self.dissued[i] = 0
        return i

    def _new_epoch_sems(self):
        for e in self.CE:
            self.esem[e] = self.new_sem("e%d_%s" % (self.epoch, e))
            self.cnt[e] = 0
        self.epoch += 1

    def buf(self, name="", dsem=False):
        b = Buf(name)
        if dsem:
            b.dsem = self.new_sem("d_" + name, True)
        self.bufs.append(b)
        return b

    def _need(self, r, w):
        need = {}

        def add(tag):
            if tag is None:
                return
            s, v = tag
            if need.get(s, 0) < v:
                need[s] = v

        for b in r:
            add(b.w)
        for b in w:
            add(b.w)
            for s, v in b.r.items():
                add((s, v))
        return need

    def _wait(self, e, need):
        for s, v in need.items():
            if self.isdma[s] and (s in self.pool_ld or s in self.pool_st):
                v = self.dissued[s]
            if self.seen[e].get(s, 0) >= v:
                continue
            self.E[e].wait_ge(self.sems[s], v)
            self.seen[e][s] = v
            self.nwait += 1

    def op(self, e, fn, r=(), w=(), sig=True):
        need = self._need(r, w)
        my = self.esem[e]
        if e == "pe":
            need.pop(my, None)
        self._wait(e, need)
        ins = fn()
        self.nins += 1
        if sig:
            ins.then_inc(self.sems[my], 1)
            self.cnt[e] += 1
            v = self.cnt[e]
            self.pending[e] = False
        else:
            assert e == "pe"
            v = self.cnt[e] + 1
            self.pending[e] = True
        for b in w:
            b.w = (my, v)
            b.r = {}
        for b in r:
            if b.r.get(my, 0) < v:
                b.r[my] = v
        return ins

    def dma(self, q, out_ap, in_ap, r=(), w=(), kind="ld"):
        need = self._need(r, w)
        s = None
        for b in w:
            if b.dsem is not None:
                s = b.dsem
        if s is None:
            if kind == "ld":
                s = self.pool_ld[self.rr_ld % len(self.pool_ld)]
                self.rr_ld += 1
            else:
                s = self.pool_st[self.rr_st % len(self.pool_st)]
                self.rr_st += 1
        if self.dissued[s] > 0 and need.get(s, 0) < self.dissued[s]:
            need[s] = self.dissued[s]
        self._wait(q, need)
        self.E[q].dma_start(out=out_ap, in_=in_ap).then_inc(self.sems[s], 16)
        self.nins += 1
        self.dissued[s] += 16
        v = self.dissued[s]
        for b in w:
            b.w = (s, v)
            b.r = {}
        for b in r:
            if b.r.get(s, 0) < v:
                b.r[s] = v

    def barrier(self, final=False):
        assert not any(self.pending.values())
        tg = {self.esem[e]: self.cnt[e] for e in self.CE if self.cnt[e] > 0}
        for s, v in self.dissued.items():
            if v > 0:
                tg[s] = v
        for e in self.E:
            need = dict(tg)
            if e == "pe":
                need.pop(self.esem["pe"], None)
            self._wait(e, need)
        if final:
            return
        for b in self.bufs:
            b.w = None
            b.r = {}


def phase_chunks(kind):
    ch = []
    if kind == "ffn":
        for m in range(22):
            ch.append(("w_in", 0, 8, m * 128, 128))
            ch.append(("w_in", 0, 8, DFF + m * 128, 128))
        for o in range(8):
            for kb, nk in ((0, 8), (1, 8), (2, 6)):
                ch.append(("w_out", kb, nk, o * 128, 128))
    elif kind == "conv":
        for fc in range(8):
            for part in range(3):
                ch.append(("w_in", 0, 8, part * D + fc * 128, 128))
        for o in range(8):
            ch.append(("w_out", 0, 8, o * 128, 128))
    elif kind == "gla":
        ch.append(("w_in", 0, 8, 3072, 16))
        for nb in range(24):
            ch.append(("w_in", 0, 8, nb * 128, 128))
        for o in range(8):
            ch.append(("w_out", 0, 8, o * 128, 128))
    elif kind == "sbA":
        for nb in range(16):
            ch.append(("w_in", 0, 8, nb * 128, 128))
        for st in range(TT // 128):
            for nb in range(16, 24):
                ch.append(("w_in", 0, 8, nb * 128, 128))
    elif kind == "sbB":
        for o in range(8):
            ch.append(("w_out", 0, 8, o * 128, 128))
    return ch


def build(npos_real, nphase, dbg=False, only=None):
    ntok = npos_real - NMETA
    NPAD = ((npos_real + TT - 1) // TT) * TT
    assert NPAD % 128 == 0
    NT128 = NPAD // 128
    nc = bass.Bass("TRN2", target_bir_lowering=False)

    def din(name, shape):
        return nc.dram_tensor(name, list(shape), F32, kind="ExternalInput").ap()

    x_d = din("x", (ntok, D))
    meta_d = din("meta_tokens", (NMETA, D))
    gains_d = din("norm_gains", (4, 6, D))
    ffn_in_d = din("ffn_w_in", (4, 2, D, 2 * DFF))
    ffn_out_d = din("ffn_w_out", (4, 2, DFF, D))
    conv_in_d = din("conv_w_in", (2, D, 3 * D))
    convw_d = din("conv_w", (2, 3, D))
    conv_out_d = din("conv_w_out", (2, D, D))
    gla_in_d = din("gla_w_in", (1, D, 3088))
    gla_gu_d = din("gla_w_gate_up", (1, 16, 512))
    gla_b_d = din("gla_b_gate", (1, 512))
    gla_n_d = din("gla_norm", (1, 256))
    gla_out_d = din("gla_w_out", (1, D, D))
    sb_in_d = din("sb_w_in", (1, D, 3 * D))
    sb_out_d = din("sb_w_out", (1, D, D))
    out_d = nc.dram_tensor("out", [ntok, D], F32, kind="ExternalOutput").ap()

    phases = []
    for l in range(4):
        phases.append(("ffn", l, 0))
        phases.append((("conv", "gla", "sb")[l % 3], l, l // 3))
        phases.append(("ffn", l, 1))
    phases = phases[:nphase]
    if only is not None:
        phases = [phases[i] for i in only]

    wsets = []
    for (kind, l, j) in phases:
        if kind == "ffn":
            wsets.append((("ffn", l, j), ["ffn"], {"w_in": ffn_in_d[l, j], "w_out": ffn_out_d[l, j]}))
        elif kind == "conv":
            wsets.append((("conv", l), ["conv"], {"w_in": conv_in_d[j], "w_out": conv_out_d[j]}))
        elif kind == "gla":
            wsets.append((("gla", l), ["gla"], {"w_in": gla_in_d[j], "w_out": gla_out_d[j]}))
        else:
            wsets.append((("sb", l), ["sbA", "sbB"], {"w_in": sb_in_d[j], "w_out": sb_out_d[j]}))

    scr = {}
    for key, kinds, srcs in wsets:
        table = {}
        for kd in kinds:
            for c in phase_chunks(kd):
                if c not in table:
                    table[c] = len(table)
        t = nc.dram_tensor("scr_%s" % "_".join(str(k) for k in key), [len(table), 128, 1024], BF16, kind="Internal").ap()
        scr[key] = (table, t, srcs)

    S = Sched(nc)

    def sb(name, shape, dt):
        return nc.alloc_sbuf_tensor(name, list(shape), dt)

    uid = [0]

    def sbt(name, shape, dt):
        uid[0] += 1
        return nc.sbuf_tensor("%s_%d" % (name, uid[0]), list(shape), dt)

    hT = sb("hT", (128, 8, NPAD), F32)
    ident = sb("ident", (128, 128), F32)
    ones_bf = sb("ones_bf", (128, 128), BF16)
    gcol = sb("gcol", (128, 8, 24), F32)
    ghalf = sb("ghalf", (128, 8, 24), F32)
    cwcol = sb("cwcol", (128, 8, 6), F32)
    gncol = sb("gncol", (128, 2), F32)
    onecol = sb("onecol", (128, 1), F32)
    epscol = sb("epscol", (128, 1), F32)
    ps = [nc.alloc_psum_tensor("ps%d" % i, [128, 512], F32) for i in range(8)]
    psb = [S.buf("ps%d" % i) for i in range(8)]
    ps_rr = [0]
    ps_avail = [list(range(8))]

    def ps_next():
        lst = ps_avail[0]
        i = lst[ps_rr[0] % len(lst)]
        ps_rr[0] += 1
        return ps[i], psb[i]

    hbufs = {}

    def hb(c, t0, T):
        k = (c, t0, T)
        if k not in hbufs:
            hbufs[k] = S.buf("h%d_%d" % (c, t0))
        return hbufs[k]

    def hbs(t0, T):
        return [hb(c, t0, T) for c in range(8)]

    V = nc.vector
    A = nc.scalar
    G = nc.gpsimd
    PE = nc.tensor

    cb = S.buf("consts")
    S.op("pool", lambda: G.memset(ident[:, :], 1.0), w=[cb])
    S.op("pool", lambda: G.affine_select(out=ident[:, :], in_=ident[:, :], pattern=[[-1, 128]],
                                         compare_op=ALU.is_ge, fill=0.0, base=0, channel_multiplier=1), r=[cb], w=[cb])
    S.op("pool", lambda: G.affine_select(out=ident[:, :], in_=ident[:, :], pattern=[[1, 128]],
                                         compare_op=ALU.is_ge, fill=0.0, base=0, channel_multiplier=-1), r=[cb], w=[cb])
    S.op("pool", lambda: G.memset(ones_bf[:, :], 1.0), w=[cb])
    S.op("pool", lambda: G.memset(onecol[:, :], 1.0), w=[cb])
    S.op("pool", lambda: G.memset(epscol[:, :], EPS), w=[cb])

    with sbt("prow", [32, D], F32) as prow, sbt("prow2", [32, D], F32) as prow2:
        pb_ = S.buf("prow")
        pb2 = S.buf("prow2")
        S.dma("sp", prow[0:24, :], gains_d.rearrange("l n d -> (l n) d"), w=[pb_])
        S.dma("sp", prow2[0:6, :], convw_d.rearrange("l t d -> (l t) d"), w=[pb2])
        S.dma("sp", prow2[6:7, 0:256], gla_n_d[0:1, :], w=[pb2])
        for c in range(8):
            pt, ptb = ps_next()
            S.op("pe", lambda: PE.transpose(out=pt[:, 0:24], in_=prow[0:24, c * 128:(c + 1) * 128], identity=ident[0:24, 0:24]),
                 r=[pb_, cb], w=[ptb])
            S.op("act", lambda: A.copy(out=gcol[:, c, :], in_=pt[:, 0:24]), r=[ptb], w=[cb])
            S.op("act", lambda: A.mul(out=ghalf[:, c, :], in_=pt[:, 0:24], mul=0.5), r=[ptb], w=[cb])
            pt2, ptb2 = ps_next()
            S.op("pe", lambda: PE.transpose(out=pt2[:, 0:6], in_=prow2[0:6, c * 128:(c + 1) * 128], identity=ident[0:6, 0:6]),
                 r=[pb2, cb], w=[ptb2])
            S.op("act", lambda: A.copy(out=cwcol[:, c, :], in_=pt2[:, 0:6]), r=[ptb2], w=[cb])
        S.dma("sp", prow[0:1, 0:256], gla_n_d[0:1, :], r=[pb_], w=[pb_])
        for vc in range(2):
            pt, ptb = ps_next()
            S.op("pe", lambda: PE.transpose(out=pt[:, 0:1], in_=prow[0:1, vc * 128:(vc + 1) * 128], identity=ident[0:1, 0:1]),
                 r=[pb_, cb], w=[ptb])
            S.op("act", lambda: A.copy(out=gncol[:, vc:vc + 1], in_=pt[:, 0:1]), r=[ptb], w=[cb])
        S.barrier()

    with sbt("st16", [128, 6, 4, 1024], BF16) as st16:
        N32 = min(8, (8 * NPAD) // 4096)
        hflat = hT[:, :, :].rearrange("p c t -> p (c t)")
        b32 = [S.buf("st32_%d" % i) for i in range(N32)]
        b16 = [S.buf("st16_%d" % i) for i in range(6)]
        it = 0
        for key, kinds, srcs in wsets[:1]:
            table, scr_ap, _ = scr[key]
            items = sorted(table.keys(), key=lambda c: (c[0], c[1], c[3]))
            i = 0
            while i < len(items):
                grp = [items[i]]
                while (len(grp) < 4 and i + len(grp) < len(items)):
                    nx = items[i + len(grp)]
                    pv = grp[-1]
                    if nx[0] == pv[0] and nx[1] == pv[1] and nx[2] == pv[2] and nx[3] == pv[3] + pv[4] and pv[4] == 128 and nx[4] == 128:
                        grp.append(nx)
                    else:
                        break
                i += len(grp)
                wname, kb, nk, n0, _ = grp[0]
                ncols = sum(g[4] for g in grp)
                src = srcs[wname][kb * 1024: kb * 1024 + nk * 128, n0:n0 + ncols].rearrange("(k p) n -> p k n", p=128)
                sl = it % N32
                s16 = it % 6
                st32v = hflat[:, sl * 4096:(sl + 1) * 4096].rearrange("p (k n) -> p k n", n=512)
                it += 1
                S.dma("sp", st32v[:, 0:nk, 0:ncols], src, w=[b32[sl]])
                eng = ("dve", "pool")[it % 2]
                if grp[0][4] == 128:
                    o_ap = st16[:, s16, 0:len(grp), 0:nk * 128].rearrange("p j (k n) -> p k j n", n=128)
                    i_ap = st32v[:, 0:nk, 0:ncols].rearrange("p k (j n) -> p k j n", n=128)
                else:
                    o_ap = st16[:, s16, 0, 0:nk * ncols].rearrange("p (k n) -> p k n", n=ncols)
                    i_ap = st32v[:, 0:nk, 0:ncols]
                if eng == "dve":
                    S.op("dve", lambda: V.tensor_copy(out=o_ap, in_=i_ap), r=[b32[sl]], w=[b16[s16]])
                else:
                    S.op("pool", lambda: G.tensor_copy(out=o_ap, in_=i_ap), r=[b32[sl]], w=[b16[s16]])
                for jj, g in enumerate(grp):
                    idx = table[g]
                    nel = nk * g[4]
                    S.dma("act", scr_ap[idx, :, 0:nel], st16[:, s16, jj, 0:nel], r=[b16[s16]], kind="st")
        S.barrier()

    with sbt("xin", [128, 2, D], F32) as xin:
        xb = [S.buf("xin0"), S.buf("xin1")]
        for j in range(NT128):
            p0 = j * 128
            sl = j % 2
            lo, hi = p0, p0 + 128
            if hi > npos_real or True:
                S.op("pool", lambda: G.memset(xin[:, sl, :], 0.0), w=[xb[sl]])
            if lo < NMETA:
                S.dma("sp", xin[0:NMETA, sl, :], meta_d[:, :], w=[xb[sl]], r=[xb[sl]])
            a = max(lo, NMETA)
            b = min(hi, npos_real)
            if b > a:
                S.dma("sp", xin[a - p0:b - p0, sl, :], x_d[a - NMETA:b - NMETA, :], w=[xb[sl]], r=[xb[sl]])
            for half in range(2):
                pt, ptb = ps_next()
                for cc in range(4):
                    c = half * 4 + cc
                    S.op("pe", lambda: PE.transpose(out=pt[:, cc * 128:(cc + 1) * 128], in_=xin[:, sl, c * 128:(c + 1) * 128],
                                                    identity=ident[:, :]), r=[xb[sl], cb], w=[ptb])
                S.op(("act", "dve")[half], lambda: (A.copy if half == 0 else V.tensor_copy)(
                    out=hT[:, half * 4:half * 4 + 4, p0:p0 + 128], in_=pt[:, :].rearrange("p (c t) -> p c t", t=128)),
                    r=[ptb], w=[hb(c, p0, 128) for c in range(half * 4, half * 4 + 4)])
        S.barrier()
    hbufs.clear()

    ring = sb("ring", (128, NRING, 1024), BF16)
    ringb = [S.buf("ring%d" % i, dsem=True) for i in range(NRING)]
    big_tiles = [(t0, TT) for t0 in range(0, NPAD, TT)]
    NREAL16 = ((npos_real + 15) // 16) * 16
    trim_tiles = [(t0, min(TT, NREAL16 - t0)) for t0 in range(0, NPAD, TT)]
    assert all(T > 0 for _, T in trim_tiles)
    small_tiles = [(t0, 128) for t0 in range(0, NPAD, 128)]
    gseq = []
    gstart = []
    for (kind, l, j) in phases:
        gstart.append(len(gseq))
        if kind == "ffn":
            table, scr_ap, _ = scr[("ffn", l, j)]
            lst = phase_chunks("ffn")
            for _t in big_tiles:
                gseq += [(scr_ap, table[c], c[2] * c[4]) for c in lst]
        elif kind == "conv":
            table, scr_ap, _ = scr[("conv", l)]
            lst = phase_chunks("conv")
            for _t in big_tiles:
                gseq += [(scr_ap, table[c], c[2] * c[4]) for c in lst]
        elif kind == "gla":
            table, scr_ap, _ = scr[("gla", l)]
            lst = phase_chunks("gla")
            inl, outl = lst[:25], lst[25:]
            assert len(outl) == 8
            seq = list(inl)
            for ti in range(len(small_tiles)):
                if ti + 1 < len(small_tiles):
                    seq += inl
                seq += outl
            gseq += [(scr_ap, table[c], c[2] * c[4]) for c in seq]
        else:
            table, scr_ap, _ = scr[("sb", l)]
            lst = phase_chunks("sbA")
            for _t in big_tiles:
                gseq += [(scr_ap, table[c], c[2] * c[4]) for c in lst]
            lst = phase_chunks("sbB")
            for _t in small_tiles:
                gseq += [(scr_ap, table[c], c[2] * c[4]) for c in lst]
    gstart.append(len(gseq))
    wpos = [0, 0]
    gend = []
    for pi_ in range(len(gstart) - 1):
        gend += [gstart[pi_ + 1]] * (gstart[pi_ + 1] - gstart[pi_])
    assert len(gend) == len(gseq)

    def wnext(ncol=128):
        i = wpos[0]
        while wpos[1] < gend[i] and wpos[1] < i + NRING:
            k = wpos[1]
            sa, idx, nel = gseq[k]
            S.dma("sp", ring[:, k % NRING, 0:nel], sa[idx, :, 0:nel], w=[ringb[k % NRING]])
            wpos[1] += 1
        wpos[0] += 1
        sl = i % NRING
        nel = gseq[i][2]
        return ring[:, sl, 0:nel].rearrange("p (k n) -> p k n", n=ncol), ringb[sl]

    BG = {"jobs": [], "t": 0, "st32": None, "st16": None, "b32": None, "b16": None}
    cur_phase = [0]
    BGN = 2

    def chunk_jobs(key):
        table, scr_ap, srcs = scr[key]
        jobs = []
        for c, idx in table.items():
            wname, kb, nk, n0, ncol = c
            halves = [(0, nk)] if nk * ncol <= 512 else [(0, nk // 2), (nk // 2, nk)]
            for k0, k1 in halves:
                src = srcs[wname][kb * 1024 + k0 * 128: kb * 1024 + k1 * 128, n0:n0 + ncol].rearrange("(k p) n -> p k n", p=128)
                jobs.append((src, scr_ap, idx, k0 * ncol, k1 - k0, ncol))
        return jobs

    bg_host = {}
    for q_ in range(1, len(phases)):
        h_ = q_ - 1
        while phases[h_][0] == "gla":
            h_ -= 1
        bg_host.setdefault(h_, []).append(q_)

    def bg_load(t):
        src, scr_ap, idx, off, nk, ncol = BG["jobs"][t]
        sl = t % BGN
        S.dma("sp", BG["st32"][:, sl, 0:nk * ncol].rearrange("p (k n) -> p k n", n=ncol), src, w=[BG["b32"][sl]])

    def bg_step():
        t = BG["t"]
        n = len(BG["jobs"])
        if t > n:
            return
        if t + 1 < n:
            bg_load(t + 1)
        if t < n:
            src, scr_ap, idx, off, nk, ncol = BG["jobs"][t]
            sl = t % BGN
            nel = nk * ncol
            S.op("dve", lambda: V.tensor_copy(out=BG["st16"][:, sl, 0:nel], in_=BG["st32"][:, sl, 0:nel]), r=[BG["b32"][sl]], w=[BG["b16"][sl]])
        if t - 1 >= 0:
            src, scr_ap, idx, off, nk, ncol = BG["jobs"][t - 1]
            sl = (t - 1) % BGN
            nel = nk * ncol
            S.dma("act", scr_ap[idx, :, off:off + nel], BG["st16"][:, sl, 0:nel], r=[BG["b16"][sl]], kind="st")
        BG["t"] = t + 1

    def bg_begin(pi, alloc):
        BG["jobs"] = []
        for q_ in bg_host.get(pi, []):
            BG["jobs"] += chunk_jobs(wsets[q_][0])
        BG["t"] = 0
        if not BG["jobs"]:
            return
        BG["st32"] = alloc("bg32", [128, BGN, 512], F32)
        BG["st16"] = alloc("bg16", [128, BGN, 512], BF16)
        if BG["b32"] is None:
            BG["b32"] = [S.buf("bg32_%d" % i, dsem=True) for i in range(BGN)]
            BG["b16"] = [S.buf("bg16_%d" % i) for i in range(BGN)]
        bg_load(0)

    def bg_tick(frac):
        n = len(BG["jobs"])
        if n == 0:
            return
        target = min(n + 1, int(np.ceil(frac * (n + 1))))
        while BG["t"] < target:
            bg_step()

    def bg_end():
        n = len(BG["jobs"])
        if n == 0:
            return
        while BG["t"] <= n:
            bg_step()
        BG["jobs"] = []

    def rstd_from_ps(pss, pssb, dst, dstb, T, inv_n):
        S.op("act", lambda: A.activation(out=dst, in_=pss, func=AF.Sqrt, bias=epscol[:, 0:1], scale=inv_n), r=[pssb, cb], w=[dstb])
        S.op("dve", lambda: V.reciprocal(out=dst, in_=dst), r=[dstb], w=[dstb])

    def pre_norm(t0, T, gidx, xn, xnb, rst, rstb):
        for c in range(8):
            S.op("pool", lambda: G.tensor_tensor(out=xn[:, c, 0:T], in0=hT[:, c, t0:t0 + T], in1=hT[:, c, t0:t0 + T], op=ALU.mult),
                 r=[hb(c, t0, T)], w=[xnb[c]])
        pss, pssb = ps_next()
        for c in range(8):
            S.op("pe", lambda: PE.matmul(pss[:, 0:T], lhsT=ones_bf[:, :], rhs=xn[:, c, 0:T], start=(c == 0), stop=(c == 7)),
                 r=[xnb[c], cb], w=[pssb], sig=(c == 7))
        rstd_from_ps(pss[:, 0:T], pssb, rst[:, 0:T], rstb, T, 1.0 / D)
        for c in range(8):
            S.op("dve", lambda: V.scalar_tensor_tensor(out=xn[:, c, 0:T], in0=hT[:, c, t0:t0 + T], scalar=gcol[:, c, gidx:gidx + 1],
                                                      in1=rst[:, 0:T], op0=ALU.mult, op1=ALU.mult),
                 r=[hb(c, t0, T), rstb, cb], w=[xnb[c]])

    def post_norm_add(t0, T, gidx, half, y, yb, ysq, ysqb, rst, rstb, tmp, tmpb):
        pss, pssb = ps_next()
        for c in range(8):
            S.op("pe", lambda: PE.matmul(pss[:, 0:T], lhsT=ones_bf[:, :], rhs=ysq[:, c, 0:T], start=(c == 0), stop=(c == 7)),
                 r=[ysqb[c], cb], w=[pssb], sig=(c == 7))
        rstd_from_ps(pss[:, 0:T], pssb, rst[:, 0:T], rstb, T, 1.0 / D)
        gsrc = ghalf if half else gcol
        for c in range(8):
            S.op("dve", lambda: V.scalar_tensor_tensor(out=tmp[:, c % 2, 0:T], in0=y[:, c, 0:T], scalar=gsrc[:, c, gidx:gidx + 1],
                                                      in1=rst[:, 0:T], op0=ALU.mult, op1=ALU.mult),
                 r=[yb[c], rstb, cb], w=[tmpb[c % 2]])
            S.op("pool", lambda: G.tensor_tensor(out=hT[:, c, t0:t0 + T], in0=hT[:, c, t0:t0 + T], in1=tmp[:, c % 2, 0:T], op=ALU.add),
                 r=[tmpb[c % 2], hb(c, t0, T)], w=[hb(c, t0, T)])

    def out_proj(T, src, srcb, y, yb, ysq, ysqb, nkb=1):
        for o in range(8):
            py, pyb = ps_next()
            first = True
            for kb in range(nkb):
                w_ap, wb = wnext()
                nk = w_ap.shape[1]
                for kk in range(nk):
                    last = (kb == nkb - 1 and kk == nk - 1)
                    k = kb * 8 + kk
                    S.op("pe", lambda: PE.matmul(py[:, 0:T], lhsT=w_ap[:, kk, :], rhs=src[:, k, 0:T], start=first, stop=last),
                         r=[wb, srcb[k]], w=[pyb], sig=last)
                    first = False
            S.op("act", lambda: A.copy(out=y[:, o, 0:T], in_=py[:, 0:T]), r=[pyb], w=[yb[o]])
            S.op("pool", lambda: G.tensor_tensor(out=ysq[:, o, 0:T], in0=y[:, o, 0:T], in1=y[:, o, 0:T], op=ALU.mult),
                 r=[yb[o]], w=[ysqb[o]])

    def run_ffn(l, j):
        g_in = l * 6 + (0 if j == 0 else 4)
        g_out = l * 6 + (1 if j == 0 else 5)
        with sbt("xn", [128, 2, 8, TT], BF16) as xn2, sbt("hid", [128, 22, TT], BF16) as hid, \
                sbt("y", [128, 8, TT], F32) as y, sbt("rst", [128, 2, TT], F32) as rst2, \
                sbt("sg", [128, 2, TT], F32) as sg, sbt("tmp", [128, 2, TT], F32) as tmp:
            xnb2 = [[S.buf("xn") for _ in range(8)] for _ in range(2)]
            hidb = [S.buf("hid") for _ in range(22)]
            yb = [S.buf("y") for _ in range(8)]
            rstb2 = [S.buf("rst"), S.buf("rst")]
            sgb = [S.buf("sg"), S.buf("sg")]
            tmpb = [S.buf("tmp"), S.buf("tmp")]
            nt = len(trim_tiles)
            with ExitStack() as bges:
                run_ffn_body(l, j, g_in, g_out, xn2, hid, y, rst2, sg, tmp, xnb2, hidb, yb, rstb2, sgb, tmpb, nt, bges)

    def run_ffn_body(l, j, g_in, g_out, xn2, hid, y, rst2, sg, tmp, xnb2, hidb, yb, rstb2, sgb, tmpb, nt, bges):
        if True:
            if True:
                bg_begin(cur_phase[0], lambda name, shape, dt: bges.enter_context(sbt(name, shape, dt)))
            pre_norm(trim_tiles[0][0], trim_tiles[0][1], g_in, xn2[:, 0], xnb2[0], rst2[:, 0], rstb2[0])
            for ti, (t0, T) in enumerate(trim_tiles):
                s = ti % 2
                xn = xn2[:, s]
                xnb = xnb2[s]
                for m in range(22):
                    pg, pgb = ps_next()
                    pu, pub = ps_next()
                    wg, wgb = wnext()
                    for k in range(8):
                        S.op("pe", lambda: PE.matmul(pg[:, 0:T], lhsT=wg[:, k, :], rhs=xn[:, k, 0:T], start=(k == 0), stop=(k == 7)),
                             r=[wgb, xnb[k]], w=[pgb], sig=(k == 7))
                    wu, wub = wnext()
                    for k in range(8):
                        S.op("pe", lambda: PE.matmul(pu[:, 0:T], lhsT=wu[:, k, :], rhs=xn[:, k, 0:T], start=(k == 0), stop=(k == 7)),
                             r=[wub, xnb[k]], w=[pub], sig=(k == 7))
                    if dbg in ("gate", "up") and m < 8:
                        pp_ = pg if dbg == "gate" else pu
                        S.op("act", lambda: A.copy(out=hT[:, m, t0:t0 + T], in_=pp_[:, 0:T]), r=[pgb, pub], w=[hb(m, t0, T)])
                    S.op("act", lambda: A.activation(out=sg[:, m % 2, 0:T], in_=pg[:, 0:T], func=AF.Silu), r=[pgb], w=[sgb[m % 2]])
                    S.op("dve", lambda: V.tensor_tensor(out=hid[:, m, 0:T], in0=sg[:, m % 2, 0:T], in1=pu[:, 0:T], op=ALU.mult),
                         r=[sgb[m % 2], pub], w=[hidb[m]])
                    bg_tick((ti * 22 + m + 1) / (nt * 22.0))
                if dbg in ("gate", "up"):
                    break
                if dbg in ("xn", "hid"):
                    for c in range(8):
                        srcap = xn[:, c, 0:T] if dbg == "xn" else hid[:, c, 0:T]
                        S.op("dve", lambda: V.tensor_copy(out=hT[:, c, t0:t0 + T], in_=srcap), r=[xnb[c], hidb[c]], w=[hb(c, t0, T)])
                    break
                if ti + 1 < nt:
                    pre_norm(trim_tiles[ti + 1][0], trim_tiles[ti + 1][1], g_in, xn2[:, 1 - s], xnb2[1 - s], rst2[:, 1 - s], rstb2[1 - s])
                out_proj(T, hid, hidb, y, yb, xn, xnb, nkb=3)
                if dbg == "y":
                    for c in range(8):
                        S.op("dve", lambda: V.tensor_copy(out=hT[:, c, t0:t0 + T], in_=y[:, c, 0:T]), r=[yb[c]], w=[hb(c, t0, T)])
                    break
                post_norm_add(t0, T, g_out, True, y, yb, xn, xnb, rst2[:, s], rstb2[s], tmp, tmpb)
            bg_end()
            S.barrier()

    def run_conv(l, j):
        g_in = l * 6 + 2
        g_out = l * 6 + 3
        with sbt("xn", [128, 2, 8, TT], BF16) as xn2, sbt("vv", [128, 8, TT], BF16) as vv, \
                sbt("y", [128, 8, TT], F32) as y, sbt("rst", [128, 2, TT], F32) as rst2, \
                sbt("csb", [128, 2, TT], F32) as csb, sbt("uext", [128, 2, TT + 2], F32) as uext, \
                sbt("acc", [128, 2, TT], F32) as acc, sbt("halo", [128, 8, 2], F32) as halo, \
                sbt("tmp", [128, 2, TT], F32) as tmp:
            xnb2 = [[S.buf("xn") for _ in range(8)] for _ in range(2)]
            vvb = [S.buf("vv") for _ in range(8)]
            yb = [S.buf("y") for _ in range(8)]
            rstb2 = [S.buf("rst"), S.buf("rst")]
            csbb = [S.buf("csb"), S.buf("csb")]
            uextb = [S.buf("uext"), S.buf("uext")]
            accb = [S.buf("acc"), S.buf("acc")]
            halob = [S.buf("halo") for _ in range(8)]
            tmpb = [S.buf("tmp"), S.buf("tmp")]
            S.op("pool", lambda: G.memset(halo[:, :, :], 0.0), w=halob)
            nt = len(trim_tiles)
            bges = ExitStack()
            bg_begin(cur_phase[0], lambda name, shape, dt: bges.enter_context(sbt(name, shape, dt)))
            pre_norm(trim_tiles[0][0], trim_tiles[0][1], g_in, xn2[:, 0], xnb2[0], rst2[:, 0], rstb2[0])
            for ti, (t0, T) in enumerate(trim_tiles):
                s = ti % 2
                xn = xn2[:, s]
                xnb = xnb2[s]
                for fc in range(8):
                    q = fc % 2
                    pbs = []
                    for part in range(3):
                        w_ap, wb = wnext()
                        pp, ppb = ps_next()
                        for k in range(8):
                            S.op("pe", lambda: PE.matmul(pp[:, 0:T], lhsT=w_ap[:, k, :], rhs=xn[:, k, 0:T], start=(k == 0), stop=(k == 7)),
                                 r=[wb, xnb[k]], w=[ppb], sig=(k == 7))
                        pbs.append((pp, ppb))
                    (pb_, pbb), (pc_, pcb), (ph_, phb) = pbs
                    S.op("act", lambda: A.copy(out=csb[:, q, 0:T], in_=pc_[:, 0:T]), r=[pcb], w=[csbb[q]])
                    S.op("act", lambda: A.copy(out=uext[:, q, 0:2], in_=halo[:, fc, :]), r=[halob[fc]], w=[uextb[q]])
                    S.op("dve", lambda: V.tensor_tensor(out=uext[:, q, 2:2 + T], in0=csb[:, q, 0:T], in1=ph_[:, 0:T], op=ALU.mult),
                         r=[csbb[q], phb, uextb[q]], w=[uextb[q]])
                    S.op("act", lambda: A.copy(out=halo[:, fc, :], in_=uext[:, q, T:T + 2]), r=[uextb[q]], w=[halob[fc]])
                    ci = l // 3 * 3
                    S.op("dve", lambda: V.tensor_scalar(out=acc[:, q, 0:T], in0=uext[:, q, 0:T], scalar1=cwcol[:, fc, ci:ci + 1], scalar2=None,
                                                       op0=ALU.mult), r=[uextb[q], cb], w=[accb[q]])
                    S.op("dve", lambda: V.scalar_tensor_tensor(out=acc[:, q, 0:T], in0=uext[:, q, 1:1 + T], scalar=cwcol[:, fc, ci + 1:ci + 2],
                                                              in1=acc[:, q, 0:T], op0=ALU.mult, op1=ALU.add), r=[uextb[q], cb, accb[q]], w=[accb[q]])
                    S.op("dve", lambda: V.scalar_tensor_tensor(out=acc[:, q, 0:T], in0=uext[:, q, 2:2 + T], scalar=cwcol[:, fc, ci + 2:ci + 3],
                                                              in1=acc[:, q, 0:T], op0=ALU.mult, op1=ALU.add), r=[uextb[q], cb, accb[q]], w=[accb[q]])
                    S.op("dve", lambda: V.tensor_tensor(out=vv[:, fc, 0:T], in0=acc[:, q, 0:T], in1=pb_[:, 0:T], op=ALU.mult),
                         r=[accb[q], pbb], w=[vvb[fc]])
                    bg_tick((ti * 8 + fc + 1) / (nt * 8.0))
                if ti + 1 < nt:
                    pre_norm(trim_tiles[ti + 1][0], trim_tiles[ti + 1][1], g_in, xn2[:, 1 - s], xnb2[1 - s], rst2[:, 1 - s], rstb2[1 - s])
                out_proj(T, vv, vvb, y, yb, xn, xnb, nkb=1)
                post_norm_add(t0, T, g_out, False, y, yb, xn, xnb, rst2[:, s], rstb2[s], tmp, tmpb)
            bg_end()
            S.barrier()
            bges.close()

    if dbg == "w":
        table, scr_ap, _ = scr[("ffn", 0, 0)]
        for ii, idx in enumerate((0, 1, 44, 46)):
            nel = 1024 if idx != 46 else 768
            nk = nel // 128
            S.dma("sp", ring[:, ii, 0:nel], scr_ap[idx, :, 0:nel], w=[ringb[ii]])
            S.op("dve", lambda: V.tensor_copy(out=hT[:, 0:nk, ii * 128:(ii + 1) * 128], in_=ring[:, ii, 0:nel].rearrange("p (k n) -> p k n", n=128)),
                 r=[ringb[ii]], w=hbs(ii * 128, 128))
        S.barrier()
        phases = []

    def run_gla(l, j):
        g_in, g_out = l * 6 + 2, l * 6 + 3
        T = 128
        with ExitStack() as es:
            def al(name, shape, dt):
                return es.enter_context(sbt(name, shape, dt))
            xn = al("gxn", [128, 8, T], BF16)
            xnb = [S.buf() for _ in range(8)]
            rst = al("grst", [128, T], F32)
            rstb = S.buf()
            a_aug = al("a_aug", [32, T], BF16)
            aab = S.buf()
            wgu32 = al("wgu32", [32, 512], F32)
            wgu = al("wgu", [32, 512], BF16)
            wgub = S.buf()
            Mincl = al("Mincl", [128, 128], BF16)
            Msuf = al("Msuf", [128, 128], BF16)
            m01 = al("m01", [128, 4, 128], F32)
            mkb = S.buf()
            Sst = al("Sst", [128, 4, 256], F32)
            Sstb = [S.buf() for _ in range(4)]
            Sbf = al("Sbf", [128, 2, 4, 256], BF16)
            Sbfb = [[S.buf() for _ in range(4)] for _ in range(2)]
            q_sb = al("q_sb", [128, 512], F32)
            kT_sb = al("kT_sb", [128, 512], F32)
            ktm_sb = al("ktm_sb", [128, 512], F32)
            qsb_b, kTb, ktmb = S.buf(), S.buf(), S.buf()
            v_sb = al("v_sb", [128, 1024], BF16)
            vb = [S.buf(), S.buf()]
            sgt = al("sgt", [128, 8, T], F32)
            sgb = [S.buf(), S.buf()]
            E32 = al("E32", [128, 512], F32)
            E32b = S.buf()
            L_sb = al("L_sb", [128, 512], BF16)
            Lb = S.buf()
            eq = al("eq", [128, 512], F32)
            ekn = al("ekn", [128, 512], F32)
            esuf = al("esuf", [128, 512], F32)
            eqb, eknb, esufb = S.buf(), S.buf(), S.buf()
            qg = al("qg", [128, 512], BF16)
            kg = al("kg", [128, 512], BF16)
            kd = al("kd", [128, 512], BF16)
            qgb, kgb, kdb = S.buf(), S.buf(), S.buf()
            attT = al("attT", [128, 512], BF16)
            attb = S.buf()
            o_sb = al("o_sb", [128, 8, T], F32)
            ob = [S.buf() for _ in range(8)]
            osq = al("osq", [128, 8, T], BF16)
            osqb = [S.buf() for _ in range(8)]
            rso = al("rso", [128, 512], F32)
            rsob = S.buf()
            og = al("og", [128, 8, T], BF16)
            ogb = [S.buf() for _ in range(8)]
            tmp = al("gtmp", [128, 2, T], F32)
            tmpb = [S.buf(), S.buf()]
            tmp2 = al("gtmp2", [128, 2, T], F32)
            tmp2b = [S.buf(), S.buf()]

            S.op("pool", lambda: G.memset(a_aug[:, :], 1.0), w=[aab])
            S.dma("sp", wgu32[0:16, :], gla_gu_d[j], w=[wgub])
            S.dma("sp", wgu32[16:17, :], gla_b_d[j:j + 1, :], r=[wgub], w=[wgub])
            S.op("dve", lambda: V.tensor_copy(out=wgu[0:17, :], in_=wgu32[0:17, :]), r=[wgub], w=[wgub])
            S.op("pool", lambda: G.memset(Mincl[:, :], -1.0 / 16), w=[mkb])
            S.op("pool", lambda: G.affine_select(out=Mincl[:, :], in_=Mincl[:, :], pattern=[[1, 128]], compare_op=ALU.is_ge, fill=0.0,
                                                 base=0, channel_multiplier=-1), r=[mkb], w=[mkb])
            S.op("pool", lambda: G.memset(Msuf[:, :], -1.0 / 16), w=[mkb])
            S.op("pool", lambda: G.affine_select(out=Msuf[:, :], in_=Msuf[:, :], pattern=[[-1, 128]], compare_op=ALU.is_gt, fill=0.0,
                                                 base=0, channel_multiplier=1), r=[mkb], w=[mkb])
            S.op("pool", lambda: G.memset(m01[:, :, :], 1.0), w=[mkb])
            S.op("pool", lambda: G.affine_select(out=m01[:, :, :], in_=m01[:, :, :], pattern=[[0, 4], [1, 128]], compare_op=ALU.is_ge, fill=0.0,
                                                 base=0, channel_multiplier=-1), r=[mkb], w=[mkb])
            S.op("pool", lambda: G.memset(Sst[:, :, :], 0.0), w=Sstb)
            S.op("pool", lambda: G.memset(Sbf[:, :, :, :], 0.0), w=Sbfb[0] + Sbfb[1])

            def mm8(dst, dstb, lhs_fn, rhs_fn, rb):
                for k in range(8):
                    S.op("pe", lambda: PE.matmul(dst, lhsT=lhs_fn(k), rhs=rhs_fn(k), start=(k == 0), stop=(k == 7)),
                         r=rb + [xnb[k]], w=[dstb], sig=(k == 7))

            def stage_A(ti):
                t0 = small_tiles[ti][0]
                pre_norm(t0, T, g_in, xn, xnb, rst, rstb)
                pa, pab = ps_next()
                w_ap, wb = wnext(ncol=16)
                mm8(pa[0:16, 0:T], pab, lambda k: w_ap[:, k, :], lambda k: xn[:, k, :], [wb])
                S.op("act", lambda: A.copy(out=a_aug[0:16, :], in_=pa[0:16, 0:T]), r=[pab], w=[aab])
                pq, pqb = ps_next()
                for hq in range(4):
                    w_ap, wb = wnext()
                    mm8(pq[:, hq * 128:(hq + 1) * 128], pqb, lambda k: w_ap[:, k, :], lambda k: xn[:, k, :], [wb])
                S.op("act", lambda: A.copy(out=q_sb[:, :], in_=pq[:, :]), r=[pqb], w=[qsb_b])
                pla, plab = ps_next()
                S.op("pe", lambda: PE.matmul(pla[:, :], lhsT=a_aug[0:17, :], rhs=wgu[0:17, :], start=True, stop=True), r=[aab, wgub], w=[plab])
                S.op("act", lambda: A.activation(out=E32[:, :], in_=pla[:, :], func=AF.Exp, scale=-1.0), r=[plab], w=[E32b])
                S.op("act", lambda: A.activation(out=L_sb[:, :], in_=E32[:, :], func=AF.Ln, bias=onecol[:, 0:1], scale=1.0), r=[E32b, cb], w=[Lb])
                pk, pkb = ps_next()
                pkt, pktb = ps_next()
                for hk in range(4):
                    w_ap, wb = wnext()
                    mm8(pk[:, hk * 128:(hk + 1) * 128], pkb, lambda k: w_ap[:, k, :], lambda k: xn[:, k, :], [wb])
                    mm8(pkt[:, hk * 128:(hk + 1) * 128], pktb, lambda k: xn[:, k, :], lambda k: w_ap[:, k, :], [wb])
                S.op("act", lambda: A.copy(out=kT_sb[:, :], in_=pk[:, :]), r=[pkb], w=[kTb])
                S.op("dve", lambda: V.tensor_copy(out=ktm_sb[:, :], in_=pkt[:, :]), r=[pktb], w=[ktmb])
                pbT, pbTb = ps_next()
                for h in range(4):
                    S.op("pe", lambda: PE.matmul(pbT[:, h * 128:(h + 1) * 128], lhsT=L_sb[:, h * 128:(h + 1) * 128], rhs=Mincl[:, :], start=True, stop=True),
                         r=[Lb, mkb], w=[pbTb])
                pbs, pbsb = ps_next()
                S.op("pe", lambda: PE.matmul(pbs[:, :], lhsT=Msuf[:, :], rhs=L_sb[:, :], start=True, stop=True), r=[Lb, mkb], w=[pbsb])
                S.op("act", lambda: A.activation(out=eq[:, :], in_=pbT[:, :], func=AF.Exp), r=[pbTb], w=[eqb])
                S.op("act", lambda: A.activation(out=ekn[:, :], in_=pbT[:, :], func=AF.Exp, scale=-1.0), r=[pbTb], w=[eknb])
                S.op("act", lambda: A.activation(out=esuf[:, :], in_=pbs[:, :], func=AF.Exp), r=[pbsb], w=[esufb])
                S.op("dve", lambda: V.scalar_tensor_tensor(out=qg[:, :], in0=q_sb[:, :], scalar=128.0 ** -0.5, in1=eq[:, :], op0=ALU.mult, op1=ALU.mult),
                     r=[qsb_b, eqb], w=[qgb])
                S.op("dve", lambda: V.tensor_tensor(out=kg[:, :], in0=kT_sb[:, :], in1=ekn[:, :], op=ALU.mult), r=[kTb, eknb], w=[kgb])
                S.op("dve", lambda: V.tensor_tensor(out=kd[:, :], in0=ktm_sb[:, :], in1=esuf[:, :], op=ALU.mult), r=[ktmb, esufb], w=[kdb])
                for half in range(2):
                    pv, pvb = ps_next()
                    for c4 in range(4):
                        w_ap, wb = wnext()
                        mm8(pv[:, c4 * 128:(c4 + 1) * 128], pvb, lambda k: xn[:, k, :], lambda k: w_ap[:, k, :], [wb])
                    if half == 0:
                        S.op("act", lambda: A.copy(out=v_sb[:, 0:512], in_=pv[:, :]), r=[pvb], w=[vb[0]])
                    else:
                        S.op("dve", lambda: V.tensor_copy(out=v_sb[:, 512:1024], in_=pv[:, :]), r=[pvb], w=[vb[1]])
                pat, patb = ps_next()
                for h in range(4):
                    S.op("pe", lambda: PE.matmul(pat[:, h * 128:(h + 1) * 128], lhsT=kg[:, h * 128:(h + 1) * 128], rhs=qg[:, h * 128:(h + 1) * 128],
                                                 start=True, stop=True), r=[kgb, qgb], w=[patb])
                S.op("dve", lambda: V.tensor_tensor(out=attT[:, :], in0=pat[:, :], in1=m01[:, :, :].rearrange("p h t -> p (h t)"), op=ALU.mult),
                     r=[patb, mkb], w=[attb])
                for half in range(2):
                    pgx, pgxb = ps_next()
                    for c4 in range(4):
                        w_ap, wb = wnext()
                        mm8(pgx[:, c4 * 128:(c4 + 1) * 128], pgxb, lambda k: w_ap[:, k, :], lambda k: xn[:, k, :], [wb])
                    S.op("act", lambda: A.activation(out=sgt[:, half * 4:half * 4 + 4, :], in_=pgx[:, :].rearrange("p (c t) -> p c t", t=128),
                                                     func=AF.Silu), r=[pgxb], w=[sgb[half]])

            def stage_B1(ti):
                so = ti % 2
                pos = [ps_next(), ps_next()]
                for h in range(4):
                    for vc in range(2):
                        c = 2 * h + vc
                        bank, bankb = pos[c // 4]
                        col = (c % 4) * 128
                        S.op("pe", lambda: PE.matmul(bank[:, col:col + 128], lhsT=v_sb[:, h * 256 + vc * 128:h * 256 + vc * 128 + 128],
                                                     rhs=attT[:, h * 128:(h + 1) * 128], start=True, stop=False),
                             r=[vb[0], vb[1], attb], w=[bankb], sig=False)
                        S.op("pe", lambda: PE.matmul(bank[:, col:col + 128], lhsT=Sbf[:, so, h, vc * 128:(vc + 1) * 128],
                                                     rhs=qg[:, h * 128:(h + 1) * 128], start=False, stop=True),
                             r=[Sbfb[so][h], qgb], w=[bankb])
                S.op("act", lambda: A.copy(out=o_sb[:, 0:4, :], in_=pos[0][0][:, :].rearrange("p (c t) -> p c t", t=128)), r=[pos[0][1]], w=ob[0:4])
                S.op("dve", lambda: V.tensor_copy(out=o_sb[:, 4:8, :], in_=pos[1][0][:, :].rearrange("p (c t) -> p c t", t=128)), r=[pos[1][1]], w=ob[4:8])
                for half in range(2):
                    S.op("pool", lambda: G.tensor_tensor(out=osq[:, half * 4:half * 4 + 4, :], in0=o_sb[:, half * 4:half * 4 + 4, :],
                                                         in1=o_sb[:, half * 4:half * 4 + 4, :], op=ALU.mult),
                         r=ob[half * 4:half * 4 + 4], w=osqb[half * 4:half * 4 + 4])
                for hp in range(2):
                    pS, pSb = ps_next()
                    for hh in range(2):
                        h = hp * 2 + hh
                        S.op("pe", lambda: PE.matmul(pS[:, hh * 256:(hh + 1) * 256], lhsT=kd[:, h * 128:(h + 1) * 128], rhs=v_sb[:, h * 256:(h + 1) * 256],
                                                     start=True, stop=True), r=[kdb, vb[0], vb[1]], w=[pSb])
                    for hh in range(2):
                        h = hp * 2 + hh
                        S.op("dve", lambda: V.scalar_tensor_tensor(out=Sst[:, h, :], in0=Sst[:, h, :], scalar=eq[:, h * 128 + 127:h * 128 + 128],
                                                                  in1=pS[:, hh * 256:(hh + 1) * 256], op0=ALU.mult, op1=ALU.add),
                             r=[Sstb[h], eqb, pSb], w=[Sstb[h]])
                        S.op("act", lambda: A.copy(out=Sbf[:, 1 - so, h, :], in_=Sst[:, h, :]), r=[Sstb[h]], w=[Sbfb[1 - so][h]])
                pss, pssb = ps_next()
                for h in range(4):
                    for vc in range(2):
                        S.op("pe", lambda: PE.matmul(pss[:, h * 128:(h + 1) * 128], lhsT=ones_bf[:, :], rhs=osq[:, 2 * h + vc, :], start=(vc == 0), stop=(vc == 1)),
                             r=[osqb[2 * h + vc], cb], w=[pssb], sig=(vc == 1))
                rstd_from_ps(pss[:, :], pssb, rso[:, :], rsob, 512, 1.0 / 256)
                for c in range(8):
                    h = c // 2
                    S.op("dve", lambda: V.scalar_tensor_tensor(out=tmp2[:, c % 2, :], in0=o_sb[:, c, :], scalar=gncol[:, c % 2:c % 2 + 1],
                                                              in1=rso[:, h * 128:(h + 1) * 128], op0=ALU.mult, op1=ALU.mult),
                         r=[ob[c], rsob, cb], w=[tmp2b[c % 2]])
                    S.op("pool", lambda: G.tensor_tensor(out=og[:, c, :], in0=tmp2[:, c % 2, :], in1=sgt[:, c, :], op=ALU.mult),
                         r=[tmp2b[c % 2], sgb[c // 4]], w=[ogb[c]])

            def stage_B2(ti):
                t0 = small_tiles[ti][0]
                out_proj(T, og, ogb, o_sb, ob, osq, osqb, nkb=1)
                post_norm_add(t0, T, g_out, False, o_sb, ob, osq, osqb, rst, rstb, tmp, tmpb)

            nt_g = len(small_tiles)
            bg_begin(cur_phase[0], al)
            stage_A(0)
            for ti in range(nt_g):
                stage_B1(ti)
                bg_tick((2 * ti + 1) / (2.0 * nt_g))
                if ti + 1 < nt_g:
                    stage_A(ti + 1)
                bg_tick((2 * ti + 2) / (2.0 * nt_g))
                stage_B2(ti)
            bg_end()
            S.barrier()

    def run_sb(l, j):
        g_in, g_out = l * 6 + 2, l * 6 + 3
        qs = nc.dram_tensor("sb_qs", [NT128, 128, 1024], BF16, kind="Internal").ap()
        ks = nc.dram_tensor("sb_ks", [NT128, 128, 1024], BF16, kind="Internal").ap()
        vs = nc.dram_tensor("sb_vs", [NT128, 128, 1024], BF16, kind="Internal").ap()
        NST = TT // 128
        with ExitStack() as es:
            def al(name, shape, dt):
                return es.enter_context(sbt(name, shape, dt))
            xn = al("sxn", [128, 8, TT], BF16)
            xnb = [S.buf() for _ in range(8)]
            rst = al("srst", [128, TT], F32)
            rstb = S.buf()
            qT_sb = al("qT_sb", [128, 8, TT], BF16)
            kT_sb = al("skT_sb", [128, 8, TT], BF16)
            vt_sb = al("vt_sb", [128, NST, 1024], BF16)
            qTb = [S.buf() for _ in range(8)]
            kTb = [S.buf() for _ in range(8)]
            vtb = [S.buf() for _ in range(NST)]
            bg_begin(cur_phase[0], al)
            nbt = len(big_tiles)
            for ti, (t0, T) in enumerate(big_tiles):
                pre_norm(t0, T, g_in, xn, xnb, rst, rstb)
                for c in range(8):
                    pq, pqb = ps_next()
                    w_ap, wb = wnext()
                    for k in range(8):
                        S.op("pe", lambda: PE.matmul(pq[:, 0:T], lhsT=w_ap[:, k, :], rhs=xn[:, k, 0:T], start=(k == 0), stop=(k == 7)),
                             r=[wb, xnb[k]], w=[pqb], sig=(k == 7))
                    S.op("act", lambda: A.mul(out=qT_sb[:, c, 0:T], in_=pq[:, 0:T], mul=0.125), r=[pqb], w=[qTb[c]])
                    bg_tick((ti * 16 + c + 1) / (nbt * 16.0))
                for c in range(8):
                    pq, pqb = ps_next()
                    w_ap, wb = wnext()
                    for k in range(8):
                        S.op("pe", lambda: PE.matmul(pq[:, 0:T], lhsT=w_ap[:, k, :], rhs=xn[:, k, 0:T], start=(k == 0), stop=(k == 7)),
                             r=[wb, xnb[k]], w=[pqb], sig=(k == 7))
                    S.op("dve", lambda: V.tensor_copy(out=kT_sb[:, c, 0:T], in_=pq[:, 0:T]), r=[pqb], w=[kTb[c]])
                    bg_tick((ti * 16 + 8 + c + 1) / (nbt * 16.0))
                for st in range(NST):
                    for half in range(2):
                        pv, pvb = ps_next()
                        for c4 in range(4):
                            w_ap, wb = wnext()
                            for k in range(8):
                                S.op("pe", lambda: PE.matmul(pv[:, c4 * 128:(c4 + 1) * 128], lhsT=xn[:, k, st * 128:(st + 1) * 128], rhs=w_ap[:, k, :],
                                                             start=(k == 0), stop=(k == 7)), r=[wb, xnb[k]], w=[pvb], sig=(k == 7))
                        if half == 0:
                            S.op("act", lambda: A.copy(out=vt_sb[:, st, 0:512], in_=pv[:, :]), r=[pvb], w=[vtb[st]])
                        else:
                            S.op("dve", lambda: V.tensor_copy(out=vt_sb[:, st, 512:1024], in_=pv[:, :]), r=[pvb, vtb[st]], w=[vtb[st]])
                for st in range(NST):
                    tl = t0 // 128 + st
                    S.dma("sp", qs[tl].rearrange("p (c t) -> p c t", t=128), qT_sb[:, :, st * 128:(st + 1) * 128], r=qTb, kind="st")
                    S.dma("sp", ks[tl].rearrange("p (c t) -> p c t", t=128), kT_sb[:, :, st * 128:(st + 1) * 128], r=kTb, kind="st")
                    S.dma("sp", vs[tl], vt_sb[:, st, :], r=[vtb[st]], kind="st")
            bg_end()
            S.barrier()
        T = 128
        if dbg == "sbA":
            return
        with ExitStack() as es:
            def al(name, shape, dt):
                return es.enter_context(sbt(name, shape, dt))
            qT = al("qT", [128, 1, 8, T], BF16)
            qTb2 = [S.buf("qTs0", dsem=True), S.buf("qTs1", dsem=True)]
            qM = al("qM", [128, 2, 2, 8, T], BF16)
            qMb = [[S.buf(), S.buf()], [S.buf(), S.buf()]]
            hmask = al("hmask", [128, 2], F32)
            hmb = S.buf()
            S.op("pool", lambda: G.memset(hmask[:, :], 1.0), w=[hmb])
            S.op("pool", lambda: G.affine_select(out=hmask[:, 0:1], in_=hmask[:, 0:1], pattern=[[0, 1]], compare_op=ALU.is_gt, fill=0.0,
                                                 base=64, channel_multiplier=-1), r=[hmb], w=[hmb])
            S.op("pool", lambda: G.affine_select(out=hmask[:, 1:2], in_=hmask[:, 1:2], pattern=[[0, 1]], compare_op=ALU.is_ge, fill=0.0,
                                                 base=-64, channel_multiplier=1), r=[hmb], w=[hmb])
            KT = al("KT", [128, 3, 8, T], BF16)
            KTb = [S.buf("KTs%d" % i, dsem=True) for i in range(3)]
            Vt = al("Vt", [128, 3, 1024], BF16)
            Vtb = [S.buf("Vts%d" % i, dsem=True) for i in range(3)]
            Ustr = al("Ustr", [128, 128], BF16)
            NegOnes = al("NegOnes", [128, 128], BF16)
            mst = al("mst", [128, 4, 128], BF16)
            mkb = S.buf()
            E32 = al("sE32", [128, 2, 512], F32)
            E32b = [S.buf(), S.buf()]
            Lbt = al("Lbt", [128, 4, 512], BF16)
            Lbb = [S.buf() for _ in range(4)]
            wT = al("wT", [128, 3, 512], BF16)
            wTb = [S.buf() for _ in range(3)]
            Lacc = al("Lacc", [128, 4, 512], BF16)
            Laccb = [S.buf() for _ in range(4)]
            o_tm = al("o_tm", [128, 1024], F32)
            otb = [S.buf(), S.buf()]
            oT = al("oT", [128, 8, T], BF16)
            oTb = [S.buf() for _ in range(8)]
            y = al("sy", [128, 8, T], F32)
            yb = [S.buf() for _ in range(8)]
            ysq = al("sysq", [128, 8, T], BF16)
            ysqb = [S.buf() for _ in range(8)]
            rst = al("srst2", [128, T], F32)
            rstb = S.buf()
            tmp = al("stmp", [128, 2, T], F32)
            tmpb = [S.buf(), S.buf()]
            S.op("pool", lambda: G.memset(Ustr[:, :], -1.0), w=[mkb])
            S.op("pool", lambda: G.affine_select(out=Ustr[:, :], in_=Ustr[:, :], pattern=[[-1, 128]], compare_op=ALU.is_ge, fill=0.0,
                                                 base=0, channel_multiplier=1), r=[mkb], w=[mkb])
            S.op("pool", lambda: G.memset(NegOnes[:, :], -1.0), w=[mkb])
            S.op("pool", lambda: G.memset(mst[:, :, :], 1.0), w=[mkb])
            S.op("pool", lambda: G.affine_select(out=mst[:, :, :], in_=mst[:, :, :], pattern=[[0, 4], [1, 128]], compare_op=ALU.is_gt, fill=0.0,
                                                 base=0, channel_multiplier=-1), r=[mkb], w=[mkb])
            if dbg == "sbB1":
                S.barrier()
                return
            ps_avail[0] = [4, 5]
            pipe_banks = [0, 1, 2, 3]
            pipe_rr = [0]

            def pipe_ps():
                i = pipe_banks[pipe_rr[0] % 4]
                pipe_rr[0] += 1
                return ps[i], psb[i]

            po = [(ps[6], psb[6]), (ps[7], psb[7])]
            items = []
            kvi = 0
            for jq, (t0, _) in enumerate(small_tiles):
                nq = max(16, min(128, NREAL16 - t0))
                for kt in range(jq, -1, -1):
                    for grp in range(4):
                        items.append(dict(jq=jq, t0=t0, kt=kt, grp=grp, diag=(kt == jq), first_tile=(kt == jq and grp == 0),
                                          first_kt=(grp == 0), last_tile=(kt == 0 and grp == 3), sl=kvi % 3, nq=nq))
                    kvi += 1
            NI = len(items)
            epq = []

            def ep_flush(tag=None):
                if tag is None:
                    n = len(epq)
                else:
                    idx = [i for i, e in enumerate(epq) if e[2] == tag]
                    n = idx[-1] + 1 if idx else 0
                for _ in range(n):
                    epq.pop(0)[1]()

            def h4(ap, nq):
                return ap.rearrange("p (h t) -> p h t", t=nq)

            def load_q(jq_):
                q2 = jq_ % 2
                S.dma("sp", qT[:, 0], qs[jq_].rearrange("p (c t) -> p c t", t=128), w=[qTb2[0]])
                for par in range(2):
                    S.op("dve", lambda: V.tensor_scalar(out=qM[:, q2, par], in0=qT[:, 0], scalar1=hmask[:, par:par + 1],
                                                        scalar2=None, op0=ALU.mult), r=[qTb2[0], hmb], w=[qMb[q2][par]])

            def st1(g, it):
                nq = it["nq"]
                qs_ = it["jq"] % 2
                if it["first_tile"]:
                    if it["jq"] == 0:
                        load_q(0)
                    if it["jq"] + 1 < len(small_tiles):
                        load_q(it["jq"] + 1)
                sl = it["sl"]
                if it["first_kt"]:
                    S.dma("sp", KT[:, sl], ks[it["kt"]].rearrange("p (c t) -> p c t", t=128), w=[KTb[sl]])
                    S.dma("sp", Vt[:, sl, :], vs[it["kt"]], w=[Vtb[sl]])
                pz, pzb = pipe_ps()
                it["pz"], it["pzb"] = pz, pzb
                for hh in range(4):
                    h = it["grp"] * 4 + hh
                    c = h // 2
                    par = h % 2
                    S.op("pe", lambda: PE.matmul(pz[:, hh * nq:(hh + 1) * nq], lhsT=KT[:, sl, c, :], rhs=qM[:, qs_, par, c, 0:nq],
                                                 start=(hh == 0), stop=False, skip_group_check=True), r=[KTb[sl], qMb[qs_][par]], w=[pzb], sig=(hh == 3))

            def st2a(g, it):
                W = 4 * it["nq"]
                e2 = g % 2
                it["e2"] = e2
                pz, pzb = it["pz"], it["pzb"]
                S.op("act", lambda: A.activation(out=E32[:, e2, 0:W], in_=pz[:, 0:W], func=AF.Exp), r=[pzb], w=[E32b[e2]])

            def st2b(g, it):
                nq = it["nq"]
                W = 4 * nq
                e2 = it["e2"]
                l5 = g % 4
                it["l5"] = l5
                S.op("act", lambda: A.activation(out=Lbt[:, l5, 0:W], in_=E32[:, e2, 0:W], func=AF.Ln, bias=onecol[:, 0:1], scale=1.0),
                     r=[E32b[e2], cb], w=[Lbb[l5]])
                if it["diag"]:
                    S.op("pool", lambda: G.tensor_tensor(out=h4(Lbt[:, l5, 0:W], nq), in0=h4(Lbt[:, l5, 0:W], nq), in1=mst[:, :, 0:nq], op=ALU.mult),
                         r=[Lbb[l5], mkb], w=[Lbb[l5]])

            def st3(g, it):
                W = 4 * it["nq"]
                l5 = it["l5"]
                grp = it["grp"]
                diag = it["diag"]
                pz, pzb = it["pz"], it["pzb"]
                S.op("pe", lambda: PE.matmul(pz[:, 0:W], lhsT=Ustr[:, :], rhs=Lbt[:, l5, 0:W], start=False, stop=diag, skip_group_check=True),
                     r=[mkb, Lbb[l5], pzb], w=[pzb], sig=diag)
                if not diag:
                    S.op("pe", lambda: PE.matmul(pz[:, 0:W], lhsT=NegOnes[:, :], rhs=Lacc[:, grp, 0:W], start=False, stop=True, skip_group_check=True),
                         r=[mkb, Laccb[grp], pzb], w=[pzb])

            def st4(g, it):
                nq = it["nq"]
                W = 4 * nq
                l5 = it["l5"]
                grp = it["grp"]
                w3 = g % 3
                it["w3"] = w3
                pz, pzb = it["pz"], it["pzb"]
                S.op("act", lambda: A.activation(out=wT[:, w3, 0:W], in_=pz[:, 0:W], func=AF.Exp), r=[pzb], w=[wTb[w3]])
                if it["diag"]:
                    S.op("pool", lambda: G.tensor_tensor(out=h4(wT[:, w3, 0:W], nq), in0=h4(wT[:, w3, 0:W], nq), in1=mst[:, :, 0:nq], op=ALU.mult),
                         r=[wTb[w3], mkb], w=[wTb[w3]])
                    S.op("pool", lambda: G.tensor_copy(out=Lacc[:, grp, 0:W], in_=Lbt[:, l5, 0:W]), r=[Lbb[l5]], w=[Laccb[grp]])
                elif it["kt"] > 0:
                    S.op("pool", lambda: G.tensor_tensor(out=Lacc[:, grp, 0:W], in0=Lacc[:, grp, 0:W], in1=Lbt[:, l5, 0:W], op=ALU.add),
                         r=[Lbb[l5], Laccb[grp]], w=[Laccb[grp]])

            def queue_epilogue(t0, nq, itn):
                bk = {}

                def s_T():
                    for half in range(2):
                        pt, ptb = ps_next()
                        for cc in range(4):
                            c = half * 4 + cc
                            S.op("pe", lambda: PE.transpose(out=pt[:, cc * 128:cc * 128 + nq], in_=o_tm[0:nq, c * 128:(c + 1) * 128], identity=ident[0:nq, 0:nq]),
                                 r=[otb[half], cb], w=[ptb])
                        S.op("dve", lambda: V.tensor_copy(out=oT[:, half * 4:half * 4 + 4, 0:nq], in_=pt[:, :].rearrange("p (c t) -> p c t", t=128)[:, :, 0:nq]),
                             r=[ptb], w=oTb[half * 4:half * 4 + 4])

                def s_proj(o):
                    def f():
                        half, oo = divmod(o, 4)
                        if oo == 0:
                            bk[("Y", half)] = ps_next()
                        py, pyb = bk[("Y", half)]
                        w_ap, wb = wnext()
                        for kk in range(8):
                            S.op("pe", lambda: PE.matmul(py[:, oo * 128:oo * 128 + nq], lhsT=w_ap[:, kk, :], rhs=oT[:, kk, 0:nq], start=(kk == 0), stop=(kk == 7)),
                                 r=[wb, oTb[kk]], w=[pyb], sig=(kk == 7))
                        if oo == 3:
                            sl4 = slice(half * 4, half * 4 + 4)
                            S.op("dve", lambda: V.tensor_copy(out=y[:, sl4, 0:nq], in_=py[:, :].rearrange("p (c t) -> p c t", t=128)[:, :, 0:nq]),
                                 r=[pyb], w=yb[sl4])
                            S.op("dve", lambda: V.tensor_tensor(out=ysq[:, sl4, 0:nq], in0=y[:, sl4, 0:nq], in1=y[:, sl4, 0:nq], op=ALU.mult),
                                 r=yb[sl4], w=ysqb[sl4])
                    return f

                def s_ss():
                    pss, pssb = ps_next()
                    bk["ss"] = (pss, pssb)
                    for c in range(8):
                        S.op("pe", lambda: PE.matmul(pss[:, 0:nq], lhsT=ones_bf[:, :], rhs=ysq[:, c, 0:nq], start=(c == 0), stop=(c == 7)),
                             r=[ysqb[c], cb], w=[pssb], sig=(c == 7))

                def s_rstd():
                    pss, pssb = bk["ss"]
                    S.op("act", lambda: A.activation(out=rst[:, 0:nq], in_=pss[:, 0:nq], func=AF.Ln, bias=epscol[:, 0:1], scale=1.0 / D), r=[pssb, cb], w=[rstb])
                    S.op("act", lambda: A.activation(out=rst[:, 0:nq], in_=rst[:, 0:nq], func=AF.Exp, scale=-0.5), r=[rstb], w=[rstb])

                def s_add():
                    for c in range(8):
                        S.op("dve", lambda: V.scalar_tensor_tensor(out=tmp[:, c % 2, 0:nq], in0=y[:, c, 0:nq], scalar=gcol[:, c, g_out:g_out + 1],
                                                                  in1=rst[:, 0:nq], op0=ALU.mult, op1=ALU.mult),
                             r=[yb[c], rstb, cb], w=[tmpb[c % 2]])
                        S.op("dve", lambda: V.tensor_tensor(out=hT[:, c, t0:t0 + nq], in0=hT[:, c, t0:t0 + nq], in1=tmp[:, c % 2, 0:nq], op=ALU.add),
                             r=[tmpb[c % 2], hb(c, t0, 128)], w=[hb(c, t0, 128)])

                epq.append([itn + 2, s_T, "T"])
                for o in range(8):
                    epq.append([itn + 4 + o, s_proj(o), "P"])
                epq.append([itn + 14, s_ss, "S"])
                epq.append([itn + 16, s_rstd, "R"])
                epq.append([itn + 17, s_add, "A"])

            def st5(g, it, itn):
                nq = it["nq"]
                w3 = it["w3"]
                sl = it["sl"]
                diag = it["diag"]
                for hh in range(4):
                    h = it["grp"] * 4 + hh
                    bank, bankb = po[h // 8]
                    S.op("pe", lambda: PE.matmul(bank[0:nq, (h % 8) * 64:(h % 8) * 64 + 64], lhsT=wT[:, w3, hh * nq:(hh + 1) * nq],
                                                 rhs=Vt[:, sl, h * 64:(h + 1) * 64], start=(diag and h % 8 == 0), stop=(it["kt"] == 0 and h % 8 == 7),
                                                 skip_group_check=True),
                         r=[wTb[w3], Vtb[sl]], w=[bankb], sig=(hh == 3))
                if it["last_tile"]:
                    ep_flush("T")
                    S.op("dve", lambda: V.tensor_copy(out=o_tm[0:nq, 0:512], in_=po[0][0][0:nq, :]), r=[po[0][1]], w=[otb[0]])
                    S.op("dve", lambda: V.tensor_copy(out=o_tm[0:nq, 512:1024], in_=po[1][0][0:nq, :]), r=[po[1][1]], w=[otb[1]])
                    queue_epilogue(it["t0"], nq, itn)

            itn = 0
            while itn < NI + 5 or epq:
                nem = 0
                while epq and epq[0][0] <= itn and nem < 2:
                    epq.pop(0)[1]()
                    nem += 1
                if 0 <= itn - 5 < NI:
                    st5(itn - 5, items[itn - 5], itn)
                if 0 <= itn - 4 < NI:
                    st4(itn - 4, items[itn - 4])
                if 0 <= itn - 3 < NI:
                    st3(itn - 3, items[itn - 3])
                if 0 <= itn - 2 < NI:
                    st2b(itn - 2, items[itn - 2])
                if 0 <= itn - 1 < NI:
                    st2a(itn - 1, items[itn - 1])
                if itn < NI:
                    st1(itn, items[itn])
                itn += 1
            S.barrier()
            ps_avail[0] = list(range(8))

    for pi_run, (kind, l, j) in enumerate(phases):
        cur_phase[0] = pi_run
        if kind == "ffn":
            run_ffn(l, j)
        elif kind == "conv":
            run_conv(l, j)
        elif kind == "gla":
            run_gla(l, j)
        else:
            run_sb(l, j)
        hbufs.clear()

    with sbt("oout", [128, 2, D], F32) as oout:
        ob = [S.buf("oo0"), S.buf("oo1")]
        for j in range(NT128):
            p0 = j * 128
            a = max(p0, NMETA)
            b = min(p0 + 128, npos_real)
            if b <= a:
                continue
            sl = j % 2
            for half in range(2):
                pt, ptb = ps_next()
                for cc in range(4):
                    c = half * 4 + cc
                    S.op("pe", lambda: PE.transpose(out=pt[:, cc * 128:(cc + 1) * 128], in_=hT[:, c, p0:p0 + 128], identity=ident[:, :]),
                         r=[hb(c, p0, 128), cb], w=[ptb])
                S.op(("act", "dve")[half], lambda: (A.copy if half == 0 else V.tensor_copy)(
                    out=oout[:, sl, half * 512:(half + 1) * 512], in_=pt[:, :]), r=[ptb], w=[ob[sl]])
            S.dma("sp", out_d[a - NMETA:b - NMETA, :], oout[a - p0:b - p0, sl, :], r=[ob[sl]], kind="st")
        S.barrier(final=True)
    print("built: ins=%d waits=%d sems=%d cnt=%s" % (S.nins, S.nwait, len(S.sems), S.cnt))
    return nc


_CACHE = {}


def kernel(**inputs):
    x = np.ascontiguousarray(inputs["x"], dtype=np.float32)
    B, L, _ = x.shape
    key = (L, 12)
    if key not in _CACHE:
        _CACHE[key] = build(L + NMETA, 12)
    nc = _CACHE[key]
    shared = {k: np.ascontiguousarray(v, dtype=np.float32) for k, v in inputs.items() if k != "x"}
    in_maps = []
    for b in range(B):
        m = dict(shared)
        m["x"] = x[b]
        in_maps.append(m)
    res = run_bass_kernel_spmd(nc, in_maps, core_ids=list(range(B)))
    return np.stack([np.asarray(r["out"]) for r in res.results], axis=0).astype(np.float32)
```

```python
import numpy as np
import concourse.bass as bass
import concourse.mybir as mybir
from concourse.bass_utils import run_bass_kernel_spmd
from contextlib import ExitStack

F32 = mybir.dt.float32
BF16 = mybir.dt.bfloat16
ALU = mybir.AluOpType
AF = mybir.ActivationFunctionType

D = 1024
DFF = 2816
NMETA = 16
EPS = 1e-6
NCORES = 8
TT = 384
NRING = 8


class Buf:
    __slots__ = ("w", "r", "dsem", "name")

    def __init__(self, name=""):
        self.w = None
        self.r = {}
        self.dsem = None
        self.name = name


class Sched:
    CE = ("pe", "act", "dve", "pool")

    def __init__(self, nc):
        self.nc = nc
        self.E = {"pe": nc.tensor, "act": nc.scalar, "dve": nc.vector, "pool": nc.gpsimd, "sp": nc.sync}
        self.sems = []
        self.isdma = []
        self.dissued = {}
        self.esem = {}
        self.cnt = {}
        self.pending = {e: False for e in self.CE}
        self.seen = {e: {} for e in self.E}
        self.bufs = []
        self.nwait = 0
        self.nins = 0
        self.pool_ld = [self.new_sem("ld%d" % i, True) for i in range(8)]
        self.pool_st = [self.new_sem("st%d" % i, True) for i in range(8)]
        self.rr_ld = 0
        self.rr_st = 0
        self.epoch = 0
        self._new_epoch_sems()

    def new_sem(self, name, dma=False):
        h = self.nc.alloc_semaphore(name)
        self.sems.append(h)
        self.isdma.append(dma)
        i = len(self.sems) - 1
        if dma:
            self.dissued[i] = 0
        return i

    def _new_epoch_sems(self):
        for e in self.CE:
            self.esem[e] = self.new_sem("e%d_%s" % (self.epoch, e))
            self.cnt[e] = 0
        self.epoch += 1

    def buf(self, name="", dsem=False):
        b = Buf(name)
        if dsem:
            b.dsem = self.new_sem("d_" + name, True)
        self.bufs.append(b)
        return b

    def _need(self, r, w):
        need = {}

        def add(tag):
            if tag is None:
                return
            s, v = tag
            if need.get(s, 0) < v:
                need[s] = v

        for b in r:
            add(b.w)
        for b in w:
            add(b.w)
            for s, v in b.r.items():
                add((s, v))
        return need

    def _wait(self, e, need):
        for s, v in need.items():
            if self.isdma[s] and (s in self.pool_ld or s in self.pool_st):
                v = self.dissued[s]
            if self.seen[e].get(s, 0) >= v:
                continue
            self.E[e].wait_ge(self.sems[s], v)
            self.seen[e][s] = v
            self.nwait += 1

    def op(self, e, fn, r=(), w=(), sig=True):
        need = self._need(r, w)
        my = self.esem[e]
        if e == "pe":
            need.pop(my, None)
        self._wait(e, need)
        ins = fn()
        self.nins += 1
        if sig:
            ins.then_inc(self.sems[my], 1)
            self.cnt[e] += 1
            v = self.cnt[e]
            self.pending[e] = False
        else:
            assert e == "pe"
            v = self.cnt[e] + 1
            self.pending[e] = True
        for b in w:
            b.w = (my, v)
            b.r = {}
        for b in r:
            if b.r.get(my, 0) < v:
                b.r[my] = v
        return ins

    def dma(self, q, out_ap, in_ap, r=(), w=(), kind="ld"):
        need = self._need(r, w)
        s = None
        for b in w:
            if b.dsem is not None:
                s = b.dsem
        if s is None:
            if kind == "ld":
                s = self.pool_ld[self.rr_ld % len(self.pool_ld)]
                self.rr_ld += 1
            else:
                s = self.pool_st[self.rr_st % len(self.pool_st)]
                self.rr_st += 1
        if self.dissued[s] > 0 and need.get(s, 0) < self.dissued[s]:
            need[s] = self.dissued[s]
        self._wait(q, need)
        self.E[q].dma_start(out=out_ap, in_=in_ap).then_inc(self.sems[s], 16)
        self.nins += 1
        self.dissued[s] += 16
        v = self.dissued[s]
        for b in w:
            b.w = (s, v)
            b.r = {}
        for b in r:
            if b.r.get(s, 0) < v:
                b.r[s] = v

    def barrier(self, final=False):
        assert not any(self.pending.values())
        tg = {self.esem[e]: self.cnt[e] for e in self.CE if self.cnt[e] > 0}
        for s, v in self.dissued.items():
            if v > 0:
                tg[s] = v
        for e in self.E:
            need = dict(tg)
            if e == "pe":
                need.pop(self.esem["pe"], None)
            self._wait(e, need)
        if final:
            return
        for b in self.bufs:
            b.w = None
            b.r = {}


def phase_chunks(kind):
    ch = []
    if kind == "ffn":
        for m in range(22):
            ch.append(("w_in", 0, 8, m * 128, 128))
            ch.append(("w_in", 0, 8, DFF + m * 128, 128))
        for o in range(8):
            for kb, nk in ((0, 8), (1, 8), (2, 6)):
                ch.append(("w_out", kb, nk, o * 128, 128))
    elif kind == "conv":
        for fc in range(8):
            for part in range(3):
                ch.append(("w_in", 0, 8, part * D + fc * 128, 128))
        for o in range(8):
            ch.append(("w_out", 0, 8, o * 128, 128))
    elif kind == "gla":
        ch.append(("w_in", 0, 8, 3072, 16))
        for nb in range(24):
            ch.append(("w_in", 0, 8, nb * 128, 128))
        for o in range(8):
            ch.append(("w_out", 0, 8, o * 128, 128))
    elif kind == "sbA":
        for nb in range(16):
            ch.append(("w_in", 0, 8, nb * 128, 128))
        for st in range(TT // 128):
            for nb in range(16, 24):
                ch.append(("w_in", 0, 8, nb * 128, 128))
    elif kind == "sbB":
        for o in range(8):
            ch.append(("w_out", 0, 8, o * 128, 128))
    return ch


def build(npos_real, nphase, dbg=False, only=None):
    ntok = npos_real - NMETA
    NPAD = ((npos_real + TT - 1) // TT) * TT
    assert NPAD % 128 == 0
    NT128 = NPAD // 128
    nc = bass.Bass("TRN2", target_bir_lowering=False)

    def din(name, shape):
        return nc.dram_tensor(name, list(shape), F32, kind="ExternalInput").ap()

    x_d = din("x", (ntok, D))
    meta_d = din("meta_tokens", (NMETA, D))
    gains_d = din("norm_gains", (4, 6, D))
    ffn_in_d = din("ffn_w_in", (4, 2, D, 2 * DFF))
    ffn_out_d = din("ffn_w_out", (4, 2, DFF, D))
    conv_in_d = din("conv_w_in", (2, D, 3 * D))
    convw_d = din("conv_w", (2, 3, D))
    conv_out_d = din("conv_w_out", (2, D, D))
    gla_in_d = din("gla_w_in", (1, D, 3088))
    gla_gu_d = din("gla_w_gate_up", (1, 16, 512))
    gla_b_d = din("gla_b_gate", (1, 512))
    gla_n_d = din("gla_norm", (1, 256))
    gla_out_d = din("gla_w_out", (1, D, D))
    sb_in_d = din("sb_w_in", (1, D, 3 * D))
    sb_out_d = din("sb_w_out", (1, D, D))
    out_d = nc.dram_tensor("out", [ntok, D], F32, kind="ExternalOutput").ap()

    phases = []
    for l in range(4):
        phases.append(("ffn", l, 0))
        phases.append((("conv", "gla", "sb")[l % 3], l, l // 3))
        phases.append(("ffn", l, 1))
    phases = phases[:nphase]
    if only is not None:
        phases = [phases[i] for i in only]

    wsets = []
    for (kind, l, j) in phases:
        if kind == "ffn":
            wsets.append((("ffn", l, j), ["ffn"], {"w_in": ffn_in_d[l, j], "w_out": ffn_out_d[l, j]}))
        elif kind == "conv":
            wsets.append((("conv", l), ["conv"], {"w_in": conv_in_d[j], "w_out": conv_out_d[j]}))
        elif kind == "gla":
            wsets.append((("gla", l), ["gla"], {"w_in": gla_in_d[j], "w_out": gla_out_d[j]}))
        else:
            wsets.append((("sb", l), ["sbA", "sbB"], {"w_in": sb_in_d[j], "w_out": sb_out_d[j]}))

    scr = {}
    for key, kinds, srcs in wsets:
        table = {}
        for kd in kinds:
            for c in phase_chunks(kd):
                if c not in table:
                    table[c] = len(table)
        t = nc.dram_tensor("scr_%s" % "_".join(str(k) for k in key), [len(table), 128, 1024], BF16, kind="Internal").ap()
        scr[key] = (table, t, srcs)

    S = Sched(nc)

    def sb(name, shape, dt):
        return nc.alloc_sbuf_tensor(name, list(shape), dt)

    uid = [0]

    def sbt(name, shape, dt):
        uid[0] += 1
        return nc.sbuf_tensor("%s_%d" % (name, uid[0]), list(shape), dt)

    hT = sb("hT", (128, 8, NPAD), F32)
    ident = sb("ident", (128, 128), F32)
    ones_bf = sb("ones_bf", (128, 128), BF16)
    gcol = sb("gcol", (128, 8, 24), F32)
    ghalf = sb("ghalf", (128, 8, 24), F32)
    cwcol = sb("cwcol", (128, 8, 6), F32)
    gncol = sb("gncol", (128, 2), F32)
    onecol = sb("onecol", (128, 1), F32)
    epscol = sb("epscol", (128, 1), F32)
    ps = [nc.alloc_psum_tensor("ps%d" % i, [128, 512], F32) for i in range(8)]
    psb = [S.buf("ps%d" % i) for i in range(8)]
    ps_rr = [0]
    ps_avail = [list(range(8))]

    def ps_next():
        lst = ps_avail[0]
        i = lst[ps_rr[0] % len(lst)]
        ps_rr[0] += 1
        return ps[i], psb[i]

    hbufs = {}

    def hb(c, t0, T):
        k = (c, t0, T)
        if k not in hbufs:
            hbufs[k] = S.buf("h%d_%d" % (c, t0))
        return hbufs[k]

    def hbs(t0, T):
        return [hb(c, t0, T) for c in range(8)]

    V = nc.vector
    A = nc.scalar
    G = nc.gpsimd
    PE = nc.tensor

    cb = S.buf("consts")
    S.op("pool", lambda: G.memset(ident[:, :], 1.0), w=[cb])
    S.op("pool", lambda: G.affine_select(out=ident[:, :], in_=ident[:, :], pattern=[[-1, 128]],
                                         compare_op=ALU.is_ge, fill=0.0, base=0, channel_multiplier=1), r=[cb], w=[cb])
    S.op("pool", lambda: G.affine_select(out=ident[:, :], in_=ident[:, :], pattern=[[1, 128]],
                                         compare_op=ALU.is_ge, fill=0.0, base=0, channel_multiplier=-1), r=[cb], w=[cb])
    S.op("pool", lambda: G.memset(ones_bf[:, :], 1.0), w=[cb])
    S.op("pool", lambda: G.memset(onecol[:, :], 1.0), w=[cb])
    S.op("pool", lambda: G.memset(epscol[:, :], EPS), w=[cb])

    with sbt("prow", [32, D], F32) as prow, sbt("prow2", [32, D], F32) as prow2:
        pb_ = S.buf("prow")
        pb2 = S.buf("prow2")
        S.dma("sp", prow[0:24, :], gains_d.rearrange("l n d -> (l n) d"), w=[pb_])
        S.dma("sp", prow2[0:6, :], convw_d.rearrange("l t d -> (l t) d"), w=[pb2])
        S.dma("sp", prow2[6:7, 0:256], gla_n_d[0:1, :], w=[pb2])
        for c in range(8):
            pt, ptb = ps_next()
            S.op("pe", lambda: PE.transpose(out=pt[:, 0:24], in_=prow[0:24, c * 128:(c + 1) * 128], identity=ident[0:24, 0:24]),
                 r=[pb_, cb], w=[ptb])
            S.op("act", lambda: A.copy(out=gcol[:, c, :], in_=pt[:, 0:24]), r=[ptb], w=[cb])
            S.op("act", lambda: A.mul(out=ghalf[:, c, :], in_=pt[:, 0:24], mul=0.5), r=[ptb], w=[cb])
            pt2, ptb2 = ps_next()
            S.op("pe", lambda: PE.transpose(out=pt2[:, 0:6], in_=prow2[0:6, c * 128:(c + 1) * 128], identity=ident[0:6, 0:6]),
                 r=[pb2, cb], w=[ptb2])
            S.op("act", lambda: A.copy(out=cwcol[:, c, :], in_=pt2[:, 0:6]), r=[ptb2], w=[cb])
        S.dma("sp", prow[0:1, 0:256], gla_n_d[0:1, :], r=[pb_], w=[pb_])
        for vc in range(2):
            pt, ptb = ps_next()
            S.op("pe", lambda: PE.transpose(out=pt[:, 0:1], in_=prow[0:1, vc * 128:(vc + 1) * 128], identity=ident[0:1, 0:1]),
                 r=[pb_, cb], w=[ptb])
            S.op("act", lambda: A.copy(out=gncol[:, vc:vc + 1], in_=pt[:, 0:1]), r=[ptb], w=[cb])
        S.barrier()

    with sbt("st16", [128, 6, 4, 1024], BF16) as st16:
        N32 = min(8, (8 * NPAD) // 4096)
        hflat = hT[:, :, :].rearrange("p c t -> p (c t)")
        b32 = [S.buf("st32_%d" % i) for i in range(N32)]
        b16 = [S.buf("st16_%d" % i) for i in range(6)]
        it = 0
        for key, kinds, srcs in wsets[:1]:
            table, scr_ap, _ = scr[key]
            items = sorted(table.keys(), key=lambda c: (c[0], c[1], c[3]))
            i = 0
            while i < len(items):
                grp = [items[i]]
                while (len(grp) < 4 and i + len(grp) < len(items)):
                    nx = items[i + len(grp)]
                    pv = grp[-1]
                    if nx[0] == pv[0] and nx[1] == pv[1] and nx[2] == pv[2] and nx[3] == pv[3] + pv[4] and pv[4] == 128 and nx[4] == 128:
                        grp.append(nx)
                    else:
                        break
                i += len(grp)
                wname, kb, nk, n0, _ = grp[0]
                ncols = sum(g[4] for g in grp)
                src = srcs[wname][kb * 1024: kb * 1024 + nk * 128, n0:n0 + ncols].rearrange("(k p) n -> p k n", p=128)
                sl = it % N32
                s16 = it % 6
                st32v = hflat[:, sl * 4096:(sl + 1) * 4096].rearrange("p (k n) -> p k n", n=512)
                it += 1
                S.dma("sp", st32v[:, 0:nk, 0:ncols], src, w=[b32[sl]])
                eng = ("dve", "pool")[it % 2]
                if grp[0][4] == 128:
                    o_ap = st16[:, s16, 0:len(grp), 0:nk * 128].rearrange("p j (k n) -> p k j n", n=128)
                    i_ap = st32v[:, 0:nk, 0:ncols].rearrange("p k (j n) -> p k j n", n=128)
                else:
                    o_ap = st16[:, s16, 0, 0:nk * ncols].rearrange("p (k n) -> p k n", n=ncols)
                    i_ap = st32v[:, 0:nk, 0:ncols]
                if eng == "dve":
                    S.op("dve", lambda: V.tensor_copy(out=o_ap, in_=i_ap), r=[b32[sl]], w=[b16[s16]])
                else:
                    S.op("pool", lambda: G.tensor_copy(out=o_ap, in_=i_ap), r=[b32[sl]], w=[b16[s16]])
                for jj, g in enumerate(grp):
                    idx = table[g]
                    nel = nk * g[4]
                    S.dma("act", scr_ap[idx, :, 0:nel], st16[:, s16, jj, 0:nel], r=[b16[s16]], kind="st")
        S.barrier()

    with sbt("xin", [128, 2, D], F32) as xin:
        xb = [S.buf("xin0"), S.buf("xin1")]
        for j in range(NT128):
            p0 = j * 128
            sl = j % 2
            lo, hi = p0, p0 + 128
            if hi > npos_real or True:
                S.op("pool", lambda: G.memset(xin[:, sl, :], 0.0), w=[xb[sl]])
            if lo < NMETA:
                S.dma("sp", xin[0:NMETA, sl, :], meta_d[:, :], w=[xb[sl]], r=[xb[sl]])
            a = max(lo, NMETA)
            b = min(hi, npos_real)
            if b > a:
                S.dma("sp", xin[a - p0:b - p0, sl, :], x_d[a - NMETA:b - NMETA, :], w=[xb[sl]], r=[xb[sl]])
            for half in range(2):
                pt, ptb = ps_next()
                for cc in range(4):
                    c = half * 4 + cc
                    S.op("pe", lambda: PE.transpose(out=pt[:, cc * 128:(cc + 1) * 128], in_=xin[:, sl, c * 128:(c + 1) * 128],
                                                    identity=ident[:, :]), r=[xb[sl], cb], w=[ptb])
                S.op(("act", "dve")[half], lambda: (A.copy if half == 0 else V.tensor_copy)(
                    out=hT[:, half * 4:half * 4 + 4, p0:p0 + 128], in_=pt[:, :].rearrange("p (c t) -> p c t", t=128)),
                    r=[ptb], w=[hb(c, p0, 128) for c in range(half * 4, half * 4 + 4)])
        S.barrier()
    hbufs.clear()

    ring = sb("ring", (128, NRING, 1024), BF16)
    ringb = [S.buf("ring%d" % i, dsem=True) for i in range(NRING)]
    big_tiles = [(t0, TT) for t0 in range(0, NPAD, TT)]
    NREAL16 = ((npos_real + 15) // 16) * 16
    trim_tiles = [(t0, min(TT, NREAL16 - t0)) for t0 in range(0, NPAD, TT)]
    assert all(T > 0 for _, T in trim_tiles)
    small_tiles = [(t0, 128) for t0 in range(0, NPAD, 128)]
    gseq = []
    gstart = []
    for (kind, l, j) in phases:
        gstart.append(len(gseq))
        if kind == "ffn":
            table, scr_ap, _ = scr[("ffn", l, j)]
            lst = phase_chunks("ffn")
            for _t in big_tiles:
                gseq += [(scr_ap, table[c], c[2] * c[4]) for c in lst]
        elif kind == "conv":
            table, scr_ap, _ = scr[("conv", l)]
            lst = phase_chunks("conv")
            for _t in big_tiles:
                gseq += [(scr_ap, table[c], c[2] * c[4]) for c in lst]
        elif kind == "gla":
            table, scr_ap, _ = scr[("gla", l)]
            lst = phase_chunks("gla")
            inl, outl = lst[:25], lst[25:]
            assert len(outl) == 8
            seq = list(inl)
            for ti in range(len(small_tiles)):
                if ti + 1 < len(small_tiles):
                    seq += inl
                seq += outl
            gseq += [(scr_ap, table[c], c[2] * c[4]) for c in seq]
        else:
            table, scr_ap, _ = scr[("sb", l)]
            lst = phase_chunks("sbA")
            for _t in big_tiles:
                gseq += [(scr_ap, table[c], c[2] * c[4]) for c in lst]
            lst = phase_chunks("sbB")
            for _t in small_tiles:
                gseq += [(scr_ap, table[c], c[2] * c[4]) for c in lst]
    gstart.append(len(gseq))
    wpos = [0, 0]
    gend = []
    for pi_ in range(len(gstart) - 1):
        gend += [gstart[pi_ + 1]] * (gstart[pi_ + 1] - gstart[pi_])
    assert len(gend) == len(gseq)

    def wnext(ncol=128):
        i = wpos[0]
        while wpos[1] < gend[i] and wpos[1] < i + NRING:
            k = wpos[1]
            sa, idx, nel = gseq[k]
            S.dma("sp", ring[:, k % NRING, 0:nel], sa[idx, :, 0:nel], w=[ringb[k % NRING]])
            wpos[1] += 1
        wpos[0] += 1
        sl = i % NRING
        nel = gseq[i][2]
        return ring[:, sl, 0:nel].rearrange("p (k n) -> p k n", n=ncol), ringb[sl]

    BG = {"jobs": [], "t": 0, "st32": None, "st16": None, "b32": None, "b16": None}
    cur_phase = [0]
    BGN = 2

    def chunk_jobs(key):
        table, scr_ap, srcs = scr[key]
        jobs = []
        for c, idx in table.items():
            wname, kb, nk, n0, ncol = c
            halves = [(0, nk)] if nk * ncol <= 512 else [(0, nk // 2), (nk // 2, nk)]
            for k0, k1 in halves:
                src = srcs[wname][kb * 1024 + k0 * 128: kb * 1024 + k1 * 128, n0:n0 + ncol].rearrange("(k p) n -> p k n", p=128)
                jobs.append((src, scr_ap, idx, k0 * ncol, k1 - k0, ncol))
        return jobs

    bg_host = {}
    for q_ in range(1, len(phases)):
        h_ = q_ - 1
        while h_ > 0 and phases[h_][0] != "ffn":
            h_ -= 1
        bg_host.setdefault(h_, []).append(q_)

    def bg_load(t):
        src, scr_ap, idx, off, nk, ncol = BG["jobs"][t]
        sl = t % BGN
        S.dma("sp", BG["st32"][:, sl, 0:nk * ncol].rearrange("p (k n) -> p k n", n=ncol), src, w=[BG["b32"][sl]])

    def bg_step():
        t = BG["t"]
        n = len(BG["jobs"])
        if t > n:
            return
        if t + 1 < n:
            bg_load(t + 1)
        if t < n:
            src, scr_ap, idx, off, nk, ncol = BG["jobs"][t]
            sl = t % BGN
            nel = nk * ncol
            S.op("dve", lambda: V.tensor_copy(out=BG["st16"][:, sl, 0:nel], in_=BG["st32"][:, sl, 0:nel]), r=[BG["b32"][sl]], w=[BG["b16"][sl]])
        if t - 1 >= 0:
            src, scr_ap, idx, off, nk, ncol = BG["jobs"][t - 1]
            sl = (t - 1) % BGN
            nel = nk * ncol
            S.dma("act", scr_ap[idx, :, off:off + nel], BG["st16"][:, sl, 0:nel], r=[BG["b16"][sl]], kind="st")
        BG["t"] = t + 1

    def bg_begin(pi, alloc):
        BG["jobs"] = []
        for q_ in bg_host.get(pi, []):
            BG["jobs"] += chunk_jobs(wsets[q_][0])
        BG["t"] = 0
        if not BG["jobs"]:
            return
        BG["st32"] = alloc("bg32", [128, BGN, 512], F32)
        BG["st16"] = alloc("bg16", [128, BGN, 512], BF16)
        if BG["b32"] is None:
            BG["b32"] = [S.buf("bg32_%d" % i, dsem=True) for i in range(BGN)]
            BG["b16"] = [S.buf("bg16_%d" % i) for i in range(BGN)]
        bg_load(0)

    def bg_tick(frac):
        n = len(BG["jobs"])
        if n == 0:
            return
        target = min(n + 1, int(np.ceil(frac * (n + 1))))
        while BG["t"] < target:
            bg_step()

    def bg_end():
        n = len(BG["jobs"])
        if n == 0:
            return
        while BG["t"] <= n:
            bg_step()
        BG["jobs"] = []

    def rstd_from_ps(pss, pssb, dst, dstb, T, inv_n):
        S.op("act", lambda: A.activation(out=dst, in_=pss, func=AF.Sqrt, bias=epscol[:, 0:1], scale=inv_n), r=[pssb, cb], w=[dstb])
        S.op("dve", lambda: V.reciprocal(out=dst, in_=dst), r=[dstb], w=[dstb])

    def pre_norm(t0, T, gidx, xn, xnb, rst, rstb):
        for c in range(8):
            S.op("pool", lambda: G.tensor_tensor(out=xn[:, c, 0:T], in0=hT[:, c, t0:t0 + T], in1=hT[:, c, t0:t0 + T], op=ALU.mult),
                 r=[hb(c, t0, T)], w=[xnb[c]])
        pss, pssb = ps_next()
        for c in range(8):
            S.op("pe", lambda: PE.matmul(pss[:, 0:T], lhsT=ones_bf[:, :], rhs=xn[:, c, 0:T], start=(c == 0), stop=(c == 7)),
                 r=[xnb[c], cb], w=[pssb], sig=(c == 7))
        rstd_from_ps(pss[:, 0:T], pssb, rst[:, 0:T], rstb, T, 1.0 / D)
        for c in range(8):
            S.op("dve", lambda: V.scalar_tensor_tensor(out=xn[:, c, 0:T], in0=hT[:, c, t0:t0 + T], scalar=gcol[:, c, gidx:gidx + 1],
                                                      in1=rst[:, 0:T], op0=ALU.mult, op1=ALU.mult),
                 r=[hb(c, t0, T), rstb, cb], w=[xnb[c]])

    def post_norm_add(t0, T, gidx, half, y, yb, ysq, ysqb, rst, rstb, tmp, tmpb):
        pss, pssb = ps_next()
        for c in range(8):
            S.op("pe", lambda: PE.matmul(pss[:, 0:T], lhsT=ones_bf[:, :], rhs=ysq[:, c, 0:T], start=(c == 0), stop=(c == 7)),
                 r=[ysqb[c], cb], w=[pssb], sig=(c == 7))
        rstd_from_ps(pss[:, 0:T], pssb, rst[:, 0:T], rstb, T, 1.0 / D)
        gsrc = ghalf if half else gcol
        for c in range(8):
            S.op("dve", lambda: V.scalar_tensor_tensor(out=tmp[:, c % 2, 0:T], in0=y[:, c, 0:T], scalar=gsrc[:, c, gidx:gidx + 1],
                                                      in1=rst[:, 0:T], op0=ALU.mult, op1=ALU.mult),
                 r=[yb[c], rstb, cb], w=[tmpb[c % 2]])
            S.op("pool", lambda: G.tensor_tensor(out=hT[:, c, t0:t0 + T], in0=hT[:, c, t0:t0 + T], in1=tmp[:, c % 2, 0:T], op=ALU.add),
                 r=[tmpb[c % 2], hb(c, t0, T)], w=[hb(c, t0, T)])

    def out_proj(T, src, srcb, y, yb, ysq, ysqb, nkb=1):
        for o in range(8):
            py, pyb = ps_next()
            first = True
            for kb in range(nkb):
                w_ap, wb = wnext()
                nk = w_ap.shape[1]
                for kk in range(nk):
                    last = (kb == nkb - 1 and kk == nk - 1)
                    k = kb * 8 + kk
                    S.op("pe", lambda: PE.matmul(py[:, 0:T], lhsT=w_ap[:, kk, :], rhs=src[:, k, 0:T], start=first, stop=last),
                         r=[wb, srcb[k]], w=[pyb], sig=last)
                    first = False
            S.op("act", lambda: A.copy(out=y[:, o, 0:T], in_=py[:, 0:T]), r=[pyb], w=[yb[o]])
            S.op("pool", lambda: G.tensor_tensor(out=ysq[:, o, 0:T], in0=y[:, o, 0:T], in1=y[:, o, 0:T], op=ALU.mult),
                 r=[yb[o]], w=[ysqb[o]])

    def run_ffn(l, j):
        g_in = l * 6 + (0 if j == 0 else 4)
        g_out = l * 6 + (1 if j == 0 else 5)
        with sbt("xn", [128, 2, 8, TT], BF16) as xn2, sbt("hid", [128, 22, TT], BF16) as hid, \
                sbt("y", [128, 8, TT], F32) as y, sbt("rst", [128, 2, TT], F32) as rst2, \
                sbt("sg", [128, 2, TT], F32) as sg, sbt("tmp", [128, 2, TT], F32) as tmp:
            xnb2 = [[S.buf("xn") for _ in range(8)] for _ in range(2)]
            hidb = [S.buf("hid") for _ in range(22)]
            yb = [S.buf("y") for _ in range(8)]
            rstb2 = [S.buf("rst"), S.buf("rst")]
            sgb = [S.buf("sg"), S.buf("sg")]
            tmpb = [S.buf("tmp"), S.buf("tmp")]
            nt = len(trim_tiles)
            with ExitStack() as bges:
                run_ffn_body(l, j, g_in, g_out, xn2, hid, y, rst2, sg, tmp, xnb2, hidb, yb, rstb2, sgb, tmpb, nt, bges)

    def run_ffn_body(l, j, g_in, g_out, xn2, hid, y, rst2, sg, tmp, xnb2, hidb, yb, rstb2, sgb, tmpb, nt, bges):
        if True:
            if True:
                bg_begin(cur_phase[0], lambda name, shape, dt: bges.enter_context(sbt(name, shape, dt)))
            pre_norm(trim_tiles[0][0], trim_tiles[0][1], g_in, xn2[:, 0], xnb2[0], rst2[:, 0], rstb2[0])
            for ti, (t0, T) in enumerate(trim_tiles):
                s = ti % 2
                xn = xn2[:, s]
                xnb = xnb2[s]
                for m in range(22):
                    pg, pgb = ps_next()
                    pu, pub = ps_next()
                    wg, wgb = wnext()
                    for k in range(8):
                        S.op("pe", lambda: PE.matmul(pg[:, 0:T], lhsT=wg[:, k, :], rhs=xn[:, k, 0:T], start=(k == 0), stop=(k == 7)),
                             r=[wgb, xnb[k]], w=[pgb], sig=(k == 7))
                    wu, wub = wnext()
                    for k in range(8):
                        S.op("pe", lambda: PE.matmul(pu[:, 0:T], lhsT=wu[:, k, :], rhs=xn[:, k, 0:T], start=(k == 0), stop=(k == 7)),
                             r=[wub, xnb[k]], w=[pub], sig=(k == 7))
                    if dbg in ("gate", "up") and m < 8:
                        pp_ = pg if dbg == "gate" else pu
                        S.op("act", lambda: A.copy(out=hT[:, m, t0:t0 + T], in_=pp_[:, 0:T]), r=[pgb, pub], w=[hb(m, t0, T)])
                    S.op("act", lambda: A.activation(out=sg[:, m % 2, 0:T], in_=pg[:, 0:T], func=AF.Silu), r=[pgb], w=[sgb[m % 2]])
                    S.op("dve", lambda: V.tensor_tensor(out=hid[:, m, 0:T], in0=sg[:, m % 2, 0:T], in1=pu[:, 0:T], op=ALU.mult),
                         r=[sgb[m % 2], pub], w=[hidb[m]])
                    bg_tick((ti * 22 + m + 1) / (nt * 22.0))
                if dbg in ("gate", "up"):
                    break
                if dbg in ("xn", "hid"):
                    for c in range(8):
                        srcap = xn[:, c, 0:T] if dbg == "xn" else hid[:, c, 0:T]
                        S.op("dve", lambda: V.tensor_copy(out=hT[:, c, t0:t0 + T], in_=srcap), r=[xnb[c], hidb[c]], w=[hb(c, t0, T)])
                    break
                if ti + 1 < nt:
                    pre_norm(trim_tiles[ti + 1][0], trim_tiles[ti + 1][1], g_in, xn2[:, 1 - s], xnb2[1 - s], rst2[:, 1 - s], rstb2[1 - s])
                out_proj(T, hid, hidb, y, yb, xn, xnb, nkb=3)
                if dbg == "y":
                    for c in range(8):
                        S.op("dve", lambda: V.tensor_copy(out=hT[:, c, t0:t0 + T], in_=y[:, c, 0:T]), r=[yb[c]], w=[hb(c, t0, T)])
                    break
                post_norm_add(t0, T, g_out, True, y, yb, xn, xnb, rst2[:, s], rstb2[s], tmp, tmpb)
            bg_end()
            S.barrier()

    def run_conv(l, j):
        g_in = l * 6 + 2
        g_out = l * 6 + 3
        with sbt("xn", [128, 2, 8, TT], BF16) as xn2, sbt("vv", [128, 8, TT], BF16) as vv, \
                sbt("y", [128, 8, TT], F32) as y, sbt("rst", [128, 2, TT], F32) as rst2, \
                sbt("csb", [128, 2, TT], F32) as csb, sbt("uext", [128, 2, TT + 2], F32) as uext, \
                sbt("acc", [128, 2, TT], F32) as acc, sbt("halo", [128, 8, 2], F32) as halo, \
                sbt("tmp", [128, 2, TT], F32) as tmp:
            xnb2 = [[S.buf("xn") for _ in range(8)] for _ in range(2)]
            vvb = [S.buf("vv") for _ in range(8)]
            yb = [S.buf("y") for _ in range(8)]
            rstb2 = [S.buf("rst"), S.buf("rst")]
            csbb = [S.buf("csb"), S.buf("csb")]
            uextb = [S.buf("uext"), S.buf("uext")]
            accb = [S.buf("acc"), S.buf("acc")]
            halob = [S.buf("halo") for _ in range(8)]
            tmpb = [S.buf("tmp"), S.buf("tmp")]
            S.op("pool", lambda: G.memset(halo[:, :, :], 0.0), w=halob)
            nt = len(trim_tiles)
            bges = ExitStack()
            bg_begin(cur_phase[0], lambda name, shape, dt: bges.enter_context(sbt(name, shape, dt)))
            pre_norm(trim_tiles[0][0], trim_tiles[0][1], g_in, xn2[:, 0], xnb2[0], rst2[:, 0], rstb2[0])
            for ti, (t0, T) in enumerate(trim_tiles):
                s = ti % 2
                xn = xn2[:, s]
                xnb = xnb2[s]
                for fc in range(8):
                    q = fc % 2
                    pbs = []
                    for part in range(3):
                        w_ap, wb = wnext()
                        pp, ppb = ps_next()
                        for k in range(8):
                            S.op("pe", lambda: PE.matmul(pp[:, 0:T], lhsT=w_ap[:, k, :], rhs=xn[:, k, 0:T], start=(k == 0), stop=(k == 7)),
                                 r=[wb, xnb[k]], w=[ppb], sig=(k == 7))
                        pbs.append((pp, ppb))
                    (pb_, pbb), (pc_, pcb), (ph_, phb) = pbs
                    S.op("act", lambda: A.copy(out=csb[:, q, 0:T], in_=pc_[:, 0:T]), r=[pcb], w=[csbb[q]])
                    S.op("act", lambda: A.copy(out=uext[:, q, 0:2], in_=halo[:, fc, :]), r=[halob[fc]], w=[uextb[q]])
                    S.op("dve", lambda: V.tensor_tensor(out=uext[:, q, 2:2 + T], in0=csb[:, q, 0:T], in1=ph_[:, 0:T], op=ALU.mult),
                         r=[csbb[q], phb, uextb[q]], w=[uextb[q]])
                    S.op("act", lambda: A.copy(out=halo[:, fc, :], in_=uext[:, q, T:T + 2]), r=[uextb[q]], w=[halob[fc]])
                    ci = l // 3 * 3
                    S.op("dve", lambda: V.tensor_scalar(out=acc[:, q, 0:T], in0=uext[:, q, 0:T], scalar1=cwcol[:, fc, ci:ci + 1], scalar2=None,
                                                       op0=ALU.mult), r=[uextb[q], cb], w=[accb[q]])
                    S.op("dve", lambda: V.scalar_tensor_tensor(out=acc[:, q, 0:T], in0=uext[:, q, 1:1 + T], scalar=cwcol[:, fc, ci + 1:ci + 2],
                                                              in1=acc[:, q, 0:T], op0=ALU.mult, op1=ALU.add), r=[uextb[q], cb, accb[q]], w=[accb[q]])
                    S.op("dve", lambda: V.scalar_tensor_tensor(out=acc[:, q, 0:T], in0=uext[:, q, 2:2 + T], scalar=cwcol[:, fc, ci + 2:ci + 3],
                                                              in1=acc[:, q, 0:T], op0=ALU.mult, op1=ALU.add), r=[uextb[q], cb, accb[q]], w=[accb[q]])
                    S.op("dve", lambda: V.tensor_tensor(out=vv[:, fc, 0:T], in0=acc[:, q, 0:T], in1=pb_[:, 0:T], op=ALU.mult),
                         r=[accb[q], pbb], w=[vvb[fc]])
                    bg_tick((ti * 8 + fc + 1) / (nt * 8.0))
                if ti + 1 < nt:
                    pre_norm(trim_tiles[ti + 1][0], trim_tiles[ti + 1][1], g_in, xn2[:, 1 - s], xnb2[1 - s], rst2[:, 1 - s], rstb2[1 - s])
                out_proj(T, vv, vvb, y, yb, xn, xnb, nkb=1)
                post_norm_add(t0, T, g_out, False, y, yb, xn, xnb, rst2[:, s], rstb2[s], tmp, tmpb)
            bg_end()
            S.barrier()
            bges.close()

    if dbg == "w":
        table, scr_ap, _ = scr[("ffn", 0, 0)]
        for ii, idx in enumerate((0, 1, 44, 46)):
            nel = 1024 if idx != 46 else 768
            nk = nel // 128
            S.dma("sp", ring[:, ii, 0:nel], scr_ap[idx, :, 0:nel], w=[ringb[ii]])
            S.op("dve", lambda: V.tensor_copy(out=hT[:, 0:nk, ii * 128:(ii + 1) * 128], in_=ring[:, ii, 0:nel].rearrange("p (k n) -> p k n", n=128)),
                 r=[ringb[ii]], w=hbs(ii * 128, 128))
        S.barrier()
        phases = []

    def run_gla(l, j):
        g_in, g_out = l * 6 + 2, l * 6 + 3
        T = 128
        with ExitStack() as es:
            def al(name, shape, dt):
                return es.enter_context(sbt(name, shape, dt))
            xn = al("gxn", [128, 8, T], BF16)
            xnb = [S.buf() for _ in range(8)]
            rst = al("grst", [128, T], F32)
            rstb = S.buf()
            a_aug = al("a_aug", [32, T], BF16)
            aab = S.buf()
            wgu32 = al("wgu32", [32, 512], F32)
            wgu = al("wgu", [32, 512], BF16)
            wgub = S.buf()
            Mincl = al("Mincl", [128, 128], BF16)
            Msuf = al("Msuf", [128, 128], BF16)
            m01 = al("m01", [128, 4, 128], F32)
            mkb = S.buf()
            Sst = al("Sst", [128, 4, 256], F32)
            Sstb = [S.buf() for _ in range(4)]
            Sbf = al("Sbf", [128, 2, 4, 256], BF16)
            Sbfb = [[S.buf() for _ in range(4)] for _ in range(2)]
            q_sb = al("q_sb", [128, 512], F32)
            kT_sb = al("kT_sb", [128, 512], F32)
            ktm_sb = al("ktm_sb", [128, 512], F32)
            qsb_b, kTb, ktmb = S.buf(), S.buf(), S.buf()
            v_sb = al("v_sb", [128, 1024], BF16)
            vb = [S.buf(), S.buf()]
            sgt = al("sgt", [128, 8, T], F32)
            sgb = [S.buf(), S.buf()]
            E32 = al("E32", [128, 512], F32)
            E32b = S.buf()
            L_sb = al("L_sb", [128, 512], BF16)
            Lb = S.buf()
            eq = al("eq", [128, 512], F32)
            ekn = al("ekn", [128, 512], F32)
            esuf = al("esuf", [128, 512], F32)
            eqb, eknb, esufb = S.buf(), S.buf(), S.buf()
            qg = al("qg", [128, 512], BF16)
            kg = al("kg", [128, 512], BF16)
            kd = al("kd", [128, 512], BF16)
            qgb, kgb, kdb = S.buf(), S.buf(), S.buf()
            attT = al("attT", [128, 512], BF16)
            attb = S.buf()
            o_sb = al("o_sb", [128, 8, T], F32)
            ob = [S.buf() for _ in range(8)]
            osq = al("osq", [128, 8, T], BF16)
            osqb = [S.buf() for _ in range(8)]
            rso = al("rso", [128, 512], F32)
            rsob = S.buf()
            og = al("og", [128, 8, T], BF16)
            ogb = [S.buf() for _ in range(8)]
            tmp = al("gtmp", [128, 2, T], F32)
            tmpb = [S.buf(), S.buf()]
            tmp2 = al("gtmp2", [128, 2, T], F32)
            tmp2b = [S.buf(), S.buf()]

            S.op("pool", lambda: G.memset(a_aug[:, :], 1.0), w=[aab])
            S.dma("sp", wgu32[0:16, :], gla_gu_d[j], w=[wgub])
            S.dma("sp", wgu32[16:17, :], gla_b_d[j:j + 1, :], r=[wgub], w=[wgub])
            S.op("dve", lambda: V.tensor_copy(out=wgu[0:17, :], in_=wgu32[0:17, :]), r=[wgub], w=[wgub])
            S.op("pool", lambda: G.memset(Mincl[:, :], -1.0 / 16), w=[mkb])
            S.op("pool", lambda: G.affine_select(out=Mincl[:, :], in_=Mincl[:, :], pattern=[[1, 128]], compare_op=ALU.is_ge, fill=0.0,
                                                 base=0, channel_multiplier=-1), r=[mkb], w=[mkb])
            S.op("pool", lambda: G.memset(Msuf[:, :], -1.0 / 16), w=[mkb])
            S.op("pool", lambda: G.affine_select(out=Msuf[:, :], in_=Msuf[:, :], pattern=[[-1, 128]], compare_op=ALU.is_gt, fill=0.0,
                                                 base=0, channel_multiplier=1), r=[mkb], w=[mkb])
            S.op("pool", lambda: G.memset(m01[:, :, :], 1.0), w=[mkb])
            S.op("pool", lambda: G.affine_select(out=m01[:, :, :], in_=m01[:, :, :], pattern=[[0, 4], [1, 128]], compare_op=ALU.is_ge, fill=0.0,
                                                 base=0, channel_multiplier=-1), r=[mkb], w=[mkb])
            S.op("pool", lambda: G.memset(Sst[:, :, :], 0.0), w=Sstb)
            S.op("pool", lambda: G.memset(Sbf[:, :, :, :], 0.0), w=Sbfb[0] + Sbfb[1])

            def mm8(dst, dstb, lhs_fn, rhs_fn, rb):
                for k in range(8):
                    S.op("pe", lambda: PE.matmul(dst, lhsT=lhs_fn(k), rhs=rhs_fn(k), start=(k == 0), stop=(k == 7)),
                         r=rb + [xnb[k]], w=[dstb], sig=(k == 7))

            def stage_A0(ti):
                pre_norm(small_tiles[ti][0], T, g_in, xn, xnb, rst, rstb)

            def stage_A(ti):
                pa, pab = ps_next()
                w_ap, wb = wnext(ncol=16)
                mm8(pa[0:16, 0:T], pab, lambda k: w_ap[:, k, :], lambda k: xn[:, k, :], [wb])
                S.op("act", lambda: A.copy(out=a_aug[0:16, :], in_=pa[0:16, 0:T]), r=[pab], w=[aab])
                pq, pqb = ps_next()
                for hq in range(4):
                    w_ap, wb = wnext()
                    mm8(pq[:, hq * 128:(hq + 1) * 128], pqb, lambda k: w_ap[:, k, :], lambda k: xn[:, k, :], [wb])
                S.op("act", lambda: A.copy(out=q_sb[:, :], in_=pq[:, :]), r=[pqb], w=[qsb_b])
                pla, plab = ps_next()
                S.op("pe", lambda: PE.matmul(pla[:, :], lhsT=a_aug[0:17, :], rhs=wgu[0:17, :], start=True, stop=True), r=[aab, wgub], w=[plab])
                S.op("act", lambda: A.activation(out=E32[:, :], in_=pla[:, :], func=AF.Exp, scale=-1.0), r=[plab], w=[E32b])
                S.op("act", lambda: A.activation(out=L_sb[:, :], in_=E32[:, :], func=AF.Ln, bias=onecol[:, 0:1], scale=1.0), r=[E32b, cb], w=[Lb])
                pk, pkb = ps_next()
                pkt, pktb = ps_next()
                for hk in range(4):
                    w_ap, wb = wnext()
                    mm8(pk[:, hk * 128:(hk + 1) * 128], pkb, lambda k: w_ap[:, k, :], lambda k: xn[:, k, :], [wb])
                    mm8(pkt[:, hk * 128:(hk + 1) * 128], pktb, lambda k: xn[:, k, :], lambda k: w_ap[:, k, :], [wb])
                S.op("act", lambda: A.copy(out=kT_sb[:, :], in_=pk[:, :]), r=[pkb], w=[kTb])
                S.op("dve", lambda: V.tensor_copy(out=ktm_sb[:, :], in_=pkt[:, :]), r=[pktb], w=[ktmb])
                pbT, pbTb = ps_next()
                for h in range(4):
                    S.op("pe", lambda: PE.matmul(pbT[:, h * 128:(h + 1) * 128], lhsT=L_sb[:, h * 128:(h + 1) * 128], rhs=Mincl[:, :], start=True, stop=True),
                         r=[Lb, mkb], w=[pbTb])
                pbs, pbsb = ps_next()
                S.op("pe", lambda: PE.matmul(pbs[:, :], lhsT=Msuf[:, :], rhs=L_sb[:, :], start=True, stop=True), r=[Lb, mkb], w=[pbsb])
                S.op("act", lambda: A.activation(out=eq[:, :], in_=pbT[:, :], func=AF.Exp), r=[pbTb], w=[eqb])
                S.op("act", lambda: A.activation(out=ekn[:, :], in_=pbT[:, :], func=AF.Exp, scale=-1.0), r=[pbTb], w=[eknb])
                S.op("act", lambda: A.activation(out=esuf[:, :], in_=pbs[:, :], func=AF.Exp), r=[pbsb], w=[esufb])
                S.op("dve", lambda: V.scalar_tensor_tensor(out=qg[:, :], in0=q_sb[:, :], scalar=128.0 ** -0.5, in1=eq[:, :], op0=ALU.mult, op1=ALU.mult),
                     r=[qsb_b, eqb], w=[qgb])
                S.op("dve", lambda: V.tensor_tensor(out=kg[:, :], in0=kT_sb[:, :], in1=ekn[:, :], op=ALU.mult), r=[kTb, eknb], w=[kgb])
                S.op("dve", lambda: V.tensor_tensor(out=kd[:, :], in0=ktm_sb[:, :], in1=esuf[:, :], op=ALU.mult), r=[ktmb, esufb], w=[kdb])
                for half in range(2):
                    pv, pvb = ps_next()
                    for c4 in range(4):
                        w_ap, wb = wnext()
                        mm8(pv[:, c4 * 128:(c4 + 1) * 128], pvb, lambda k: xn[:, k, :], lambda k: w_ap[:, k, :], [wb])
                    if half == 0:
                        S.op("act", lambda: A.copy(out=v_sb[:, 0:512], in_=pv[:, :]), r=[pvb], w=[vb[0]])
                    else:
                        S.op("dve", lambda: V.tensor_copy(out=v_sb[:, 512:1024], in_=pv[:, :]), r=[pvb], w=[vb[1]])
                pat, patb = ps_next()
                for h in range(4):
                    S.op("pe", lambda: PE.matmul(pat[:, h * 128:(h + 1) * 128], lhsT=kg[:, h * 128:(h + 1) * 128], rhs=qg[:, h * 128:(h + 1) * 128],
                                                 start=True, stop=True), r=[kgb, qgb], w=[patb])
                S.op("dve", lambda: V.tensor_tensor(out=attT[:, :], in0=pat[:, :], in1=m01[:, :, :].rearrange("p h t -> p (h t)"), op=ALU.mult),
                     r=[patb, mkb], w=[attb])
                for half in range(2):
                    pgx, pgxb = ps_next()
                    for c4 in range(4):
                        w_ap, wb = wnext()
                        mm8(pgx[:, c4 * 128:(c4 + 1) * 128], pgxb, lambda k: w_ap[:, k, :], lambda k: xn[:, k, :], [wb])
                    S.op("act", lambda: A.activation(out=sgt[:, half * 4:half * 4 + 4, :], in_=pgx[:, :].rearrange("p (c t) -> p c t", t=128),
                                                     func=AF.Silu), r=[pgxb], w=[sgb[half]])

            def stage_B1(ti):
                so = ti % 2
                pos = [ps_next(), ps_next()]
                for h in range(4):
                    for vc in range(2):
                        c = 2 * h + vc
                        bank, bankb = pos[c // 4]
                        col = (c % 4) * 128
                        S.op("pe", lambda: PE.matmul(bank[:, col:col + 128], lhsT=v_sb[:, h * 256 + vc * 128:h * 256 + vc * 128 + 128],
                                                     rhs=attT[:, h * 128:(h + 1) * 128], start=True, stop=False),
                             r=[vb[0], vb[1], attb], w=[bankb], sig=False)
                        S.op("pe", lambda: PE.matmul(bank[:, col:col + 128], lhsT=Sbf[:, so, h, vc * 128:(vc + 1) * 128],
                                                     rhs=qg[:, h * 128:(h + 1) * 128], start=False, stop=True),
                             r=[Sbfb[so][h], qgb], w=[bankb])
                S.op("act", lambda: A.copy(out=o_sb[:, 0:4, :], in_=pos[0][0][:, :].rearrange("p (c t) -> p c t", t=128)), r=[pos[0][1]], w=ob[0:4])
                S.op("dve", lambda: V.tensor_copy(out=o_sb[:, 4:8, :], in_=pos[1][0][:, :].rearrange("p (c t) -> p c t", t=128)), r=[pos[1][1]], w=ob[4:8])
                for half in range(2):
                    S.op("pool", lambda: G.tensor_tensor(out=osq[:, half * 4:half * 4 + 4, :], in0=o_sb[:, half * 4:half * 4 + 4, :],
                                                         in1=o_sb[:, half * 4:half * 4 + 4, :], op=ALU.mult),
                         r=ob[half * 4:half * 4 + 4], w=osqb[half * 4:half * 4 + 4])
                for hp in range(2):
                    pS, pSb = ps_next()
                    for hh in range(2):
                        h = hp * 2 + hh
                        S.op("pe", lambda: PE.matmul(pS[:, hh * 256:(hh + 1) * 256], lhsT=kd[:, h * 128:(h + 1) * 128], rhs=v_sb[:, h * 256:(h + 1) * 256],
                                                     start=True, stop=True), r=[kdb, vb[0], vb[1]], w=[pSb])
                    for hh in range(2):
                        h = hp * 2 + hh
                        S.op("dve", lambda: V.scalar_tensor_tensor(out=Sst[:, h, :], in0=Sst[:, h, :], scalar=eq[:, h * 128 + 127:h * 128 + 128],
                                                                  in1=pS[:, hh * 256:(hh + 1) * 256], op0=ALU.mult, op1=ALU.add),
                             r=[Sstb[h], eqb, pSb], w=[Sstb[h]])
                        S.op("act", lambda: A.copy(out=Sbf[:, 1 - so, h, :], in_=Sst[:, h, :]), r=[Sstb[h]], w=[Sbfb[1 - so][h]])
                pss, pssb = ps_next()
                for h in range(4):
                    for vc in range(2):
                        S.op("pe", lambda: PE.matmul(pss[:, h * 128:(h + 1) * 128], lhsT=ones_bf[:, :], rhs=osq[:, 2 * h + vc, :], start=(vc == 0), stop=(vc == 1)),
                             r=[osqb[2 * h + vc], cb], w=[pssb], sig=(vc == 1))
                rstd_from_ps(pss[:, :], pssb, rso[:, :], rsob, 512, 1.0 / 256)
                for c in range(8):
                    h = c // 2
                    S.op("dve", lambda: V.scalar_tensor_tensor(out=tmp2[:, c % 2, :], in0=o_sb[:, c, :], scalar=gncol[:, c % 2:c % 2 + 1],
                                                              in1=rso[:, h * 128:(h + 1) * 128], op0=ALU.mult, op1=ALU.mult),
                         r=[ob[c], rsob, cb], w=[tmp2b[c % 2]])
                    S.op("pool", lambda: G.tensor_tensor(out=og[:, c, :], in0=tmp2[:, c % 2, :], in1=sgt[:, c, :], op=ALU.mult),
                         r=[tmp2b[c % 2], sgb[c // 4]], w=[ogb[c]])

            def stage_B2(ti):
                t0 = small_tiles[ti][0]
                out_proj(T, og, ogb, o_sb, ob, osq, osqb, nkb=1)
                post_norm_add(t0, T, g_out, False, o_sb, ob, osq, osqb, rst, rstb, tmp, tmpb)

            nt_g = len(small_tiles)
            bg_begin(cur_phase[0], al)
            stage_A0(0)
            stage_A(0)
            if nt_g > 1:
                stage_A0(1)
            for ti in range(nt_g):
                stage_B1(ti)
                bg_tick((2 * ti + 1) / (2.0 * nt_g))
                if ti + 1 < nt_g:
                    stage_A(ti + 1)
                if ti + 2 < nt_g:
                    stage_A0(ti + 2)
                bg_tick((2 * ti + 2) / (2.0 * nt_g))
                stage_B2(ti)
            bg_end()
            S.barrier()

    def run_sb(l, j):
        g_in, g_out = l * 6 + 2, l * 6 + 3
        qs = nc.dram_tensor("sb_qs", [NT128, 128, 1024], BF16, kind="Internal").ap()
        ks = nc.dram_tensor("sb_ks", [NT128, 128, 1024], BF16, kind="Internal").ap()
        vs = nc.dram_tensor("sb_vs", [NT128, 128, 1024], BF16, kind="Internal").ap()
        NST = TT // 128
        with ExitStack() as es:
            def al(name, shape, dt):
                return es.enter_context(sbt(name, shape, dt))
            xn = al("sxn", [128, 8, TT], BF16)
            xnb = [S.buf() for _ in range(8)]
            rst = al("srst", [128, TT], F32)
            rstb = S.buf()
            qT_sb = al("qT_sb", [128, 8, TT], BF16)
            kT_sb = al("skT_sb", [128, 8, TT], BF16)
            vt_sb = al("vt_sb", [128, NST, 1024], BF16)
            qTb = [S.buf() for _ in range(8)]
            kTb = [S.buf() for _ in range(8)]
            vtb = [S.buf() for _ in range(NST)]
            bg_begin(cur_phase[0], al)
            nbt = len(big_tiles)
            for ti, (t0, T) in enumerate(big_tiles):
                pre_norm(t0, T, g_in, xn, xnb, rst, rstb)
                for c in range(8):
                    pq, pqb = ps_next()
                    w_ap, wb = wnext()
                    for k in range(8):
                        S.op("pe", lambda: PE.matmul(pq[:, 0:T], lhsT=w_ap[:, k, :], rhs=xn[:, k, 0:T], start=(k == 0), stop=(k == 7)),
                             r=[wb, xnb[k]], w=[pqb], sig=(k == 7))
                    S.op("act", lambda: A.mul(out=qT_sb[:, c, 0:T], in_=pq[:, 0:T], mul=0.125), r=[pqb], w=[qTb[c]])
                    bg_tick((ti * 16 + c + 1) / (nbt * 16.0))
                for c in range(8):
                    pq, pqb = ps_next()
                    w_ap, wb = wnext()
                    for k in range(8):
                        S.op("pe", lambda: PE.matmul(pq[:, 0:T], lhsT=w_ap[:, k, :], rhs=xn[:, k, 0:T], start=(k == 0), stop=(k == 7)),
                             r=[wb, xnb[k]], w=[pqb], sig=(k == 7))
                    S.op("dve", lambda: V.tensor_copy(out=kT_sb[:, c, 0:T], in_=pq[:, 0:T]), r=[pqb], w=[kTb[c]])
                    bg_tick((ti * 16 + 8 + c + 1) / (nbt * 16.0))
                for st in range(NST):
                    for half in range(2):
                        pv, pvb = ps_next()
                        for c4 in range(4):
                            w_ap, wb = wnext()
                            for k in range(8):
                                S.op("pe", lambda: PE.matmul(pv[:, c4 * 128:(c4 + 1) * 128], lhsT=xn[:, k, st * 128:(st + 1) * 128], rhs=w_ap[:, k, :],
                                                             start=(k == 0), stop=(k == 7)), r=[wb, xnb[k]], w=[pvb], sig=(k == 7))
                        if half == 0:
                            S.op("act", lambda: A.copy(out=vt_sb[:, st, 0:512], in_=pv[:, :]), r=[pvb], w=[vtb[st]])
                        else:
                            S.op("dve", lambda: V.tensor_copy(out=vt_sb[:, st, 512:1024], in_=pv[:, :]), r=[pvb, vtb[st]], w=[vtb[st]])
                for st in range(NST):
                    tl = t0 // 128 + st
                    S.dma("sp", qs[tl].rearrange("p (c t) -> p c t", t=128), qT_sb[:, :, st * 128:(st + 1) * 128], r=qTb, kind="st")
                    S.dma("sp", ks[tl].rearrange("p (c t) -> p c t", t=128), kT_sb[:, :, st * 128:(st + 1) * 128], r=kTb, kind="st")
                    S.dma("sp", vs[tl], vt_sb[:, st, :], r=[vtb[st]], kind="st")
            bg_end()
            S.barrier()
        T = 128
        if dbg == "sbA":
            return
        with ExitStack() as es:
            def al(name, shape, dt):
                return es.enter_context(sbt(name, shape, dt))
            qT = al("qT", [128, 1, 8, T], BF16)
            qTb2 = [S.buf("qTs0", dsem=True), S.buf("qTs1", dsem=True)]
            qM = al("qM", [128, 2, 2, 8, T], BF16)
            qMb = [[S.buf(), S.buf()], [S.buf(), S.buf()]]
            hmask = al("hmask", [128, 2], F32)
            hmb = S.buf()
            S.op("pool", lambda: G.memset(hmask[:, :], 1.0), w=[hmb])
            S.op("pool", lambda: G.affine_select(out=hmask[:, 0:1], in_=hmask[:, 0:1], pattern=[[0, 1]], compare_op=ALU.is_gt, fill=0.0,
                                                 base=64, channel_multiplier=-1), r=[hmb], w=[hmb])
            S.op("pool", lambda: G.affine_select(out=hmask[:, 1:2], in_=hmask[:, 1:2], pattern=[[0, 1]], compare_op=ALU.is_ge, fill=0.0,
                                                 base=-64, channel_multiplier=1), r=[hmb], w=[hmb])
            KT = al("KT", [128, 3, 8, T], BF16)
            KTb = [S.buf("KTs%d" % i, dsem=True) for i in range(3)]
            Vt = al("Vt", [128, 3, 1024], BF16)
            Vtb = [S.buf("Vts%d" % i, dsem=True) for i in range(3)]
            Ustr = al("Ustr", [128, 128], BF16)
            NegOnes = al("NegOnes", [128, 128], BF16)
            mst = al("mst", [128, 4, 128], BF16)
            mkb = S.buf()
            E32 = al("sE32", [128, 2, 512], F32)
            E32b = [S.buf(), S.buf()]
            Lbt = al("Lbt", [128, 4, 512], BF16)
            Lbb = [S.buf() for _ in range(4)]
            wT = al("wT", [128, 3, 512], BF16)
            wTb = [S.buf() for _ in range(3)]
            Lacc = al("Lacc", [128, 4, 512], BF16)
            Laccb = [S.buf() for _ in range(4)]
            o_tm = al("o_tm", [128, 1024], F32)
            otb = [S.buf(), S.buf()]
            oT = al("oT", [128, 8, T], BF16)
            oTb = [S.buf() for _ in range(8)]
            y = al("sy", [128, 8, T], F32)
            yb = [S.buf() for _ in range(8)]
            ysq = al("sysq", [128, 8, T], BF16)
            ysqb = [S.buf() for _ in range(8)]
            rst = al("srst2", [128, T], F32)
            rstb = S.buf()
            tmp = al("stmp", [128, 2, T], F32)
            tmpb = [S.buf(), S.buf()]
            S.op("pool", lambda: G.memset(Ustr[:, :], -1.0), w=[mkb])
            S.op("pool", lambda: G.affine_select(out=Ustr[:, :], in_=Ustr[:, :], pattern=[[-1, 128]], compare_op=ALU.is_ge, fill=0.0,
                                                 base=0, channel_multiplier=1), r=[mkb], w=[mkb])
            S.op("pool", lambda: G.memset(NegOnes[:, :], -1.0), w=[mkb])
            S.op("pool", lambda: G.memset(mst[:, :, :], 1.0), w=[mkb])
            S.op("pool", lambda: G.affine_select(out=mst[:, :, :], in_=mst[:, :, :], pattern=[[0, 4], [1, 128]], compare_op=ALU.is_gt, fill=0.0,
                                                 base=0, channel_multiplier=-1), r=[mkb], w=[mkb])
            if dbg == "sbB1":
                S.barrier()
                return
            ps_avail[0] = [4, 5]
            pipe_banks = [0, 1, 2, 3]
            pipe_rr = [0]

            def pipe_ps():
                i = pipe_banks[pipe_rr[0] % 4]
                pipe_rr[0] += 1
                return ps[i], psb[i]

            po = [(ps[6], psb[6]), (ps[7], psb[7])]
            items = []
            kvi = 0
            for jq, (t0, _) in enumerate(small_tiles):
                nq = max(16, min(128, NREAL16 - t0))
                for kt in range(jq, -1, -1):
                    for grp in range(4):
                        items.append(dict(jq=jq, t0=t0, kt=kt, grp=grp, diag=(kt == jq), first_tile=(kt == jq and grp == 0),
                                          first_kt=(grp == 0), last_tile=(kt == 0 and grp == 3), sl=kvi % 3, nq=nq))
                    kvi += 1
            NI = len(items)
            epq = []

            def ep_flush(tag=None):
                if tag is None:
                    n = len(epq)
                else:
                    idx = [i for i, e in enumerate(epq) if e[2] == tag]
                    n = idx[-1] + 1 if idx else 0
                for _ in range(n):
                    epq.pop(0)[1]()

            def h4(ap, nq):
                return ap.rearrange("p (h t) -> p h t", t=nq)

            def load_q(jq_):
                q2 = jq_ % 2
                S.dma("sp", qT[:, 0], qs[jq_].rearrange("p (c t) -> p c t", t=128), w=[qTb2[0]])
                for par in range(2):
                    S.op("dve", lambda: V.tensor_scalar(out=qM[:, q2, par], in0=qT[:, 0], scalar1=hmask[:, par:par + 1],
                                                        scalar2=None, op0=ALU.mult), r=[qTb2[0], hmb], w=[qMb[q2][par]])

            def st1(g, it):
                nq = it["nq"]
                qs_ = it["jq"] % 2
                if it["first_tile"]:
                    if it["jq"] == 0:
                        load_q(0)
                    if it["jq"] + 1 < len(small_tiles):
                        load_q(it["jq"] + 1)
                sl = it["sl"]
                if it["first_kt"]:
                    S.dma("sp", KT[:, sl], ks[it["kt"]].rearrange("p (c t) -> p c t", t=128), w=[KTb[sl]])
                    S.dma("sp", Vt[:, sl, :], vs[it["kt"]], w=[Vtb[sl]])
                pz, pzb = pipe_ps()
                it["pz"], it["pzb"] = pz, pzb
                for hh in range(4):
                    h = it["grp"] * 4 + hh
                    c = h // 2
                    par = h % 2
                    S.op("pe", lambda: PE.matmul(pz[:, hh * nq:(hh + 1) * nq], lhsT=KT[:, sl, c, :], rhs=qM[:, qs_, par, c, 0:nq],
                                                 start=(hh == 0), stop=False, skip_group_check=True), r=[KTb[sl], qMb[qs_][par]], w=[pzb], sig=(hh == 3))

            def st2a(g, it):
                W = 4 * it["nq"]
                e2 = g % 2
                it["e2"] = e2
                pz, pzb = it["pz"], it["pzb"]
                S.op("act", lambda: A.activation(out=E32[:, e2, 0:W], in_=pz[:, 0:W], func=AF.Exp), r=[pzb], w=[E32b[e2]])

            def st2b(g, it):
                nq = it["nq"]
                W = 4 * nq
                e2 = it["e2"]
                l5 = g % 4
                it["l5"] = l5
                S.op("act", lambda: A.activation(out=Lbt[:, l5, 0:W], in_=E32[:, e2, 0:W], func=AF.Ln, bias=onecol[:, 0:1], scale=1.0),
                     r=[E32b[e2], cb], w=[Lbb[l5]])
                if it["diag"]:
                    S.op("pool", lambda: G.tensor_tensor(out=h4(Lbt[:, l5, 0:W], nq), in0=h4(Lbt[:, l5, 0:W], nq), in1=mst[:, :, 0:nq], op=ALU.mult),
                         r=[Lbb[l5], mkb], w=[Lbb[l5]])

            def st3(g, it):
                W = 4 * it["nq"]
                l5 = it["l5"]
                grp = it["grp"]
                diag = it["diag"]
                pz, pzb = it["pz"], it["pzb"]
                S.op("pe", lambda: PE.matmul(pz[:, 0:W], lhsT=Ustr[:, :], rhs=Lbt[:, l5, 0:W], start=False, stop=diag, skip_group_check=True),
                     r=[mkb, Lbb[l5], pzb], w=[pzb], sig=diag)
                if not diag:
                    S.op("pe", lambda: PE.matmul(pz[:, 0:W], lhsT=NegOnes[:, :], rhs=Lacc[:, grp, 0:W], start=False, stop=True, skip_group_check=True),
                         r=[mkb, Laccb[grp], pzb], w=[pzb])

            def st4(g, it):
                nq = it["nq"]
                W = 4 * nq
                l5 = it["l5"]
                grp = it["grp"]
                w3 = g % 3
                it["w3"] = w3
                pz, pzb = it["pz"], it["pzb"]
                S.op("act", lambda: A.activation(out=wT[:, w3, 0:W], in_=pz[:, 0:W], func=AF.Exp), r=[pzb], w=[wTb[w3]])
                if it["diag"]:
                    S.op("pool", lambda: G.tensor_tensor(out=h4(wT[:, w3, 0:W], nq), in0=h4(wT[:, w3, 0:W], nq), in1=mst[:, :, 0:nq], op=ALU.mult),
                         r=[wTb[w3], mkb], w=[wTb[w3]])
                    S.op("pool", lambda: G.tensor_copy(out=Lacc[:, grp, 0:W], in_=Lbt[:, l5, 0:W]), r=[Lbb[l5]], w=[Laccb[grp]])
                elif it["kt"] > 0:
                    S.op("pool", lambda: G.tensor_tensor(out=Lacc[:, grp, 0:W], in0=Lacc[:, grp, 0:W], in1=Lbt[:, l5, 0:W], op=ALU.add),
                         r=[Lbb[l5], Laccb[grp]], w=[Laccb[grp]])

            def queue_epilogue(t0, nq, itn):
                bk = {}

                def s_T():
                    for half in range(2):
                        pt, ptb = ps_next()
                        for cc in range(4):
                            c = half * 4 + cc
                            S.op("pe", lambda: PE.transpose(out=pt[:, cc * 128:cc * 128 + nq], in_=o_tm[0:nq, c * 128:(c + 1) * 128], identity=ident[0:nq, 0:nq]),
                                 r=[otb[half], cb], w=[ptb])
                        S.op("dve", lambda: V.tensor_copy(out=oT[:, half * 4:half * 4 + 4, 0:nq], in_=pt[:, :].rearrange("p (c t) -> p c t", t=128)[:, :, 0:nq]),
                             r=[ptb], w=oTb[half * 4:half * 4 + 4])

                def s_proj(o):
                    def f():
                        half, oo = divmod(o, 4)
                        if oo == 0:
                            bk[("Y", half)] = ps_next()
                        py, pyb = bk[("Y", half)]
                        w_ap, wb = wnext()
                        for kk in range(8):
                            S.op("pe", lambda: PE.matmul(py[:, oo * 128:oo * 128 + nq], lhsT=w_ap[:, kk, :], rhs=oT[:, kk, 0:nq], start=(kk == 0), stop=(kk == 7)),
                                 r=[wb, oTb[kk]], w=[pyb], sig=(kk == 7))
                        if oo == 3:
                            sl4 = slice(half * 4, half * 4 + 4)
                            S.op("dve", lambda: V.tensor_copy(out=y[:, sl4, 0:nq], in_=py[:, :].rearrange("p (c t) -> p c t", t=128)[:, :, 0:nq]),
                                 r=[pyb], w=yb[sl4])
                            S.op("dve", lambda: V.tensor_tensor(out=ysq[:, sl4, 0:nq], in0=y[:, sl4, 0:nq], in1=y[:, sl4, 0:nq], op=ALU.mult),
                                 r=yb[sl4], w=ysqb[sl4])
                    return f

                def s_ss():
                    pss, pssb = ps_next()
                    bk["ss"] = (pss, pssb)
                    for c in range(8):
                        S.op("pe", lambda: PE.matmul(pss[:, 0:nq], lhsT=ones_bf[:, :], rhs=ysq[:, c, 0:nq], start=(c == 0), stop=(c == 7)),
                             r=[ysqb[c], cb], w=[pssb], sig=(c == 7))

                def s_rstd():
                    pss, pssb = bk["ss"]
                    S.op("act", lambda: A.activation(out=rst[:, 0:nq], in_=pss[:, 0:nq], func=AF.Ln, bias=epscol[:, 0:1], scale=1.0 / D), r=[pssb, cb], w=[rstb])
                    S.op("act", lambda: A.activation(out=rst[:, 0:nq], in_=rst[:, 0:nq], func=AF.Exp, scale=-0.5), r=[rstb], w=[rstb])

                def s_add():
                    for c in range(8):
                        S.op("dve", lambda: V.scalar_tensor_tensor(out=tmp[:, c % 2, 0:nq], in0=y[:, c, 0:nq], scalar=gcol[:, c, g_out:g_out + 1],
                                                                  in1=rst[:, 0:nq], op0=ALU.mult, op1=ALU.mult),
                             r=[yb[c], rstb, cb], w=[tmpb[c % 2]])
                        S.op("dve", lambda: V.tensor_tensor(out=hT[:, c, t0:t0 + nq], in0=hT[:, c, t0:t0 + nq], in1=tmp[:, c % 2, 0:nq], op=ALU.add),
                             r=[tmpb[c % 2], hb(c, t0, 128)], w=[hb(c, t0, 128)])

                epq.append([itn + 2, s_T, "T"])
                for o in range(8):
                    epq.append([itn + 4 + o, s_proj(o), "P"])
                epq.append([itn + 14, s_ss, "S"])
                epq.append([itn + 16, s_rstd, "R"])
                epq.append([itn + 17, s_add, "A"])

            def st5(g, it, itn):
                nq = it["nq"]
                w3 = it["w3"]
                sl = it["sl"]
                diag = it["diag"]
                for hh in range(4):
                    h = it["grp"] * 4 + hh
                    bank, bankb = po[h // 8]
                    S.op("pe", lambda: PE.matmul(bank[0:nq, (h % 8) * 64:(h % 8) * 64 + 64], lhsT=wT[:, w3, hh * nq:(hh + 1) * nq],
                                                 rhs=Vt[:, sl, h * 64:(h + 1) * 64], start=(diag and h % 8 == 0), stop=(it["kt"] == 0 and h % 8 == 7),
                                                 skip_group_check=True),
                         r=[wTb[w3], Vtb[sl]], w=[bankb], sig=(hh == 3))
                if it["last_tile"]:
                    ep_flush("T")
                    S.op("dve", lambda: V.tensor_copy(out=o_tm[0:nq, 0:512], in_=po[0][0][0:nq, :]), r=[po[0][1]], w=[otb[0]])
                    S.op("dve", lambda: V.tensor_copy(out=o_tm[0:nq, 512:1024], in_=po[1][0][0:nq, :]), r=[po[1][1]], w=[otb[1]])
                    queue_epilogue(it["t0"], nq, itn)

            itn = 0
            while itn < NI + 5 or epq:
                nem = 0
                while epq and epq[0][0] <= itn and nem < 2:
                    epq.pop(0)[1]()
                    nem += 1
                if 0 <= itn - 5 < NI:
                    st5(itn - 5, items[itn - 5], itn)
                if 0 <= itn - 4 < NI:
                    st4(itn - 4, items[itn - 4])
                if 0 <= itn - 3 < NI:
                    st3(itn - 3, items[itn - 3])
                if 0 <= itn - 2 < NI:
                    st2b(itn - 2, items[itn - 2])
                if 0 <= itn - 1 < NI:
                    st2a(itn - 1, items[itn - 1])
                if itn < NI:
                    st1(itn, items[itn])
                itn += 1
            S.barrier()
            ps_avail[0] = list(range(8))

    for pi_run, (kind, l, j) in enumerate(phases):
        cur_phase[0] = pi_run
        if kind == "ffn":
            run_ffn(l, j)
        elif kind == "conv":
            run_conv(l, j)
        elif kind == "gla":
            run_gla(l, j)
        else:
            run_sb(l, j)
        hbufs.clear()

    with sbt("oout", [128, 2, D], F32) as oout:
        ob = [S.buf("oo0"), S.buf("oo1")]
        for j in range(NT128):
            p0 = j * 128
            a = max(p0, NMETA)
            b = min(p0 + 128, npos_real)
            if b <= a:
                continue
            sl = j % 2
            for half in range(2):
                pt, ptb = ps_next()
                for cc in range(4):
                    c = half * 4 + cc
                    S.op("pe", lambda: PE.transpose(out=pt[:, cc * 128:(cc + 1) * 128], in_=hT[:, c, p0:p0 + 128], identity=ident[:, :]),
                         r=[hb(c, p0, 128), cb], w=[ptb])
                S.op(("act", "dve")[half], lambda: (A.copy if half == 0 else V.tensor_copy)(
                    out=oout[:, sl, half * 512:(half + 1) * 512], in_=pt[:, :]), r=[ptb], w=[ob[sl]])
            S.dma("sp", out_d[a - NMETA:b - NMETA, :], oout[a - p0:b - p0, sl, :], r=[ob[sl]], kind="st")
        S.barrier(final=True)
    print("built: ins=%d waits=%d sems=%d cnt=%s" % (S.nins, S.nwait, len(S.sems), S.cnt))
    return nc


_CACHE = {}


def kernel(**inputs):
    x = np.ascontiguousarray(inputs["x"], dtype=np.float32)
    B, L, _ = x.shape
    key = (L, 12)
    if key not in _CACHE:
        _CACHE[key] = build(L + NMETA, 12)
    nc = _CACHE[key]
    shared = {k: np.ascontiguousarray(v, dtype=np.float32) for k, v in inputs.items() if k != "x"}
    in_maps = []
    for b in range(B):
        m = dict(shared)
        m["x"] = x[b]
        in_maps.append(m)
    res = run_bass_kernel_spmd(nc, in_maps, core_ids=list(range(B)))
    return np.stack([np.asarray(r["out"]) for r in res.results], axis=0).astype(np.float32)
```
